# Optimizing a Trainium2 kernel written in Bass

```python
import jax, jax.numpy as jnp
from jax import lax
import numpy as np

D_MODEL = 1024
BATCH = 4
SEQ = 4096
DEPTH = 1

MEM_LEN = 256
GRID_W = 64
EPS = 1e-6
HEAD_DIM = 64
ATTN_WIDTH = D_MODEL // 2
N_Q_HEADS = ATTN_WIDTH // HEAD_DIM
N_KV_HEADS = N_Q_HEADS // 4
Q_PER_KV = N_Q_HEADS // N_KV_HEADS
KV_WIDTH = N_KV_HEADS * HEAD_DIM
Q_BLOCK = 128
ROPE_THETA = 10000.0
ROPE_AXIS_DIM = HEAD_DIM // 2
ROPE_NFREQ = ROPE_AXIS_DIM // 2
GMLP_WIDTH = D_MODEL // 2
GMLP_GROUPS = 4
GMLP_GROUP_DIM = GMLP_WIDTH // GMLP_GROUPS
GMLP_CHUNK = 128
MEM_HEADS = 4
MEM_HEAD_DIM = 128
MEM_WIDTH = MEM_HEADS * MEM_HEAD_DIM
N_BRANCH = 3
IN_WIDTH = ATTN_WIDTH + 2 * KV_WIDTH + 2 * GMLP_WIDTH + MEM_WIDTH
D_FF = 2816

kernel_name = "hybrid_gated_gqa_gmlp_memory_macaron"


def rmsnorm(x, g):
    xf = x.astype(jnp.float32)
    y = xf * lax.rsqrt(jnp.mean(xf * xf, axis=-1, keepdims=True) + EPS)
    return (y * g.astype(jnp.float32)).astype(x.dtype)


def swiglu(h, w_gate, w_up, w_down):
    return (jax.nn.silu(h @ w_gate) * (h @ w_up)) @ w_down


def axial_rope_tables(seq, dtype):
    rows = seq // GRID_W
    row = jnp.repeat(jnp.arange(rows, dtype=jnp.float32), GRID_W)
    col = jnp.tile(jnp.arange(GRID_W, dtype=jnp.float32), rows)
    inv_freq = ROPE_THETA ** (-jnp.arange(ROPE_NFREQ, dtype=jnp.float32) / ROPE_NFREQ)
    ang = jnp.stack([row[:, None] * inv_freq, col[:, None] * inv_freq], axis=1)
    return jnp.cos(ang).astype(dtype), jnp.sin(ang).astype(dtype)


def apply_axial_rope(x, cos, sin):
    b, s, h, d = x.shape
    xr = x.reshape(b, s, h, 2, 2, ROPE_NFREQ)
    x1, x2 = xr[..., 0, :], xr[..., 1, :]
    c, sn = cos[None, :, None], sin[None, :, None]
    out = jnp.stack([x1 * c - x2 * sn, x2 * c + x1 * sn], axis=-2)
    return out.reshape(b, s, h, d)


def gqa_blocks(q, k, v):
    b, s, _, d = q.shape
    nq = s // Q_BLOCK
    scale = d ** -0.5
    qb = q.reshape(b, nq, Q_BLOCK, N_KV_HEADS, Q_PER_KV, d).transpose(1, 0, 2, 3, 4, 5)

    def attend(q_blk):
        sc = jnp.einsum('bqkgd,bskd->bkgqs', q_blk, k).astype(jnp.float32) * scale
        p = jax.nn.softmax(sc, axis=-1).astype(v.dtype)
        return jnp.einsum('bkgqs,bskd->bqkgd', p, v)

    o = lax.map(attend, qb)
    return o.transpose(1, 0, 2, 3, 4, 5).reshape(b, s, N_Q_HEADS * d)


def gmlp_spatial_gate(u, v, v_norm, w_s, b_s):
    b, s, _ = v.shape
    nc = s // GMLP_CHUNK
    vn = rmsnorm(v, v_norm).reshape(b, nc, GMLP_CHUNK, GMLP_GROUPS, GMLP_GROUP_DIM)
    mixed = jnp.einsum('gpq,bnqgc->bnpgc', w_s, vn) + b_s.T[None, None, :, :, None]
    return u * mixed.reshape(b, s, GMLP_WIDTH)


def memory_cross_attention(qm, km, vm):
    b, s, _, d = qm.shape
    sc = jnp.einsum('bshd,bmhd->bhsm', qm, km).astype(jnp.float32) * (d ** -0.5)
    p = jax.nn.softmax(sc, axis=-1).astype(vm.dtype)
    return jnp.einsum('bhsm,bmhd->bshd', p, vm).reshape(b, s, MEM_WIDTH)


def setup_inputs(seed: int = 0) -> dict:
    key = jax.random.key(seed)
    ks = iter(jax.random.split(key, 40))
    f32 = jnp.float32

    def w(shape, fan_in):
        return jax.random.normal(next(ks), shape, f32) * (fan_in ** -0.5)

    def gain(shape):
        return 1.0 + 0.05 * jax.random.normal(next(ks), shape, f32)

    def bias(shape):
        return 0.01 * jax.random.normal(next(ks), shape, f32)

    L = DEPTH
    return {
        "x": jax.random.normal(next(ks), (BATCH, SEQ, D_MODEL), f32),
        "mem": jax.random.normal(next(ks), (BATCH, MEM_LEN, D_MODEL), f32),
        "ffn1_pre": gain((L, D_MODEL)),
        "ffn1_w_gate": w((L, D_MODEL, D_FF), D_MODEL),
        "ffn1_w_up": w((L, D_MODEL, D_FF), D_MODEL),
        "ffn1_w_down": w((L, D_FF, D_MODEL), D_FF),
        "ffn1_post": gain((L, D_MODEL)),
        "mix_pre": gain((L, D_MODEL)),
        "mem_norm": gain((L, D_MODEL)),
        "w_in": w((L, D_MODEL, IN_WIDTH), D_MODEL),
        "w_mem_kv": w((L, D_MODEL, 2 * MEM_WIDTH), D_MODEL),
        "q_norm": gain((L, HEAD_DIM)),
        "k_norm": gain((L, HEAD_DIM)),
        "gmlp_v_norm": gain((L, GMLP_WIDTH)),
        "gmlp_w_s": w((L, GMLP_GROUPS, GMLP_CHUNK, GMLP_CHUNK), GMLP_CHUNK),
        "gmlp_b_s": 1.0 + bias((L, GMLP_GROUPS, GMLP_CHUNK)),
        "w_branch_gate": w((L, D_MODEL, N_BRANCH * D_MODEL), D_MODEL),
        "b_branch_gate": bias((L, N_BRANCH * D_MODEL)),
        "w_proj_attn": w((L, ATTN_WIDTH, D_MODEL), ATTN_WIDTH),
        "w_proj_gmlp": w((L, GMLP_WIDTH, D_MODEL), GMLP_WIDTH),
        "w_proj_mem": w((L, MEM_WIDTH, D_MODEL), MEM_WIDTH),
        "w_out": w((L, D_MODEL, D_MODEL), D_MODEL),
        "mix_post": gain((L, D_MODEL)),
        "ffn2_pre": gain((L, D_MODEL)),
        "ffn2_w_gate": w((L, D_MODEL, D_FF), D_MODEL),
        "ffn2_w_up": w((L, D_MODEL, D_FF), D_MODEL),
        "ffn2_w_down": w((L, D_FF, D_MODEL), D_FF),
        "ffn2_post": gain((L, D_MODEL)),
    }


def reference(x, mem, ffn1_pre, ffn1_w_gate, ffn1_w_up, ffn1_w_down, ffn1_post,
              mix_pre, mem_norm, w_in, w_mem_kv, q_norm, k_norm, gmlp_v_norm,
              gmlp_w_s, gmlp_b_s, w_branch_gate, b_branch_gate, w_proj_attn,
              w_proj_gmlp, w_proj_mem, w_out, mix_post, ffn2_pre, ffn2_w_gate,
              ffn2_w_up, ffn2_w_down, ffn2_post):
    b, s, _ = x.shape
    cos, sin = axial_rope_tables(s, x.dtype)
    splits = np.cumsum([ATTN_WIDTH, KV_WIDTH, KV_WIDTH, GMLP_WIDTH, GMLP_WIDTH]).tolist()

    for l in range(DEPTH):
        h = rmsnorm(x, ffn1_pre[l])
        x = x + 0.5 * rmsnorm(swiglu(h, ffn1_w_gate[l], ffn1_w_up[l], ffn1_w_down[l]), ffn1_post[l])

        h = rmsnorm(x, mix_pre[l])
        z = h @ w_in[l]
        q, k, v, gu, gv, qm = jnp.split(z, splits, axis=-1)

        q = apply_axial_rope(rmsnorm(q.reshape(b, s, N_Q_HEADS, HEAD_DIM), q_norm[l]), cos, sin)
        k = apply_axial_rope(rmsnorm(k.reshape(b, s, N_KV_HEADS, HEAD_DIM), k_norm[l]), cos, sin)
        v = v.reshape(b, s, N_KV_HEADS, HEAD_DIM)
        br_attn = gqa_blocks(q, k, v) @ w_proj_attn[l]

        gu = jax.nn.gelu(gu)
        gv = jax.nn.gelu(gv)
        br_gmlp = gmlp_spatial_gate(gu, gv, gmlp_v_norm[l], gmlp_w_s[l], gmlp_b_s[l]) @ w_proj_gmlp[l]

        kvm = rmsnorm(mem, mem_norm[l]) @ w_mem_kv[l]
        km, vm = jnp.split(kvm.reshape(b, MEM_LEN, 2, MEM_HEADS, MEM_HEAD_DIM), 2, axis=2)
        qm = qm.reshape(b, s, MEM_HEADS, MEM_HEAD_DIM)
        br_mem = memory_cross_attention(qm, km[:, :, 0], vm[:, :, 0]) @ w_proj_mem[l]

        gates = jax.nn.sigmoid(h @ w_branch_gate[l] + b_branch_gate[l]).reshape(b, s, N_BRANCH, D_MODEL)
        merged = gates[:, :, 0] * br_attn + gates[:, :, 1] * br_gmlp + gates[:, :, 2] * br_mem
        x = x + rmsnorm(merged @ w_out[l], mix_post[l])

        h = rmsnorm(x, ffn2_pre[l])
        x = x + 0.5 * rmsnorm(swiglu(h, ffn2_w_gate[l], ffn2_w_up[l], ffn2_w_down[l]), ffn2_post[l])
    return x
```

```python
import numpy as np
import concourse.bass as bass
import concourse.mybir as mybir
from concourse.bass_utils import run_bass_kernel_spmd
from contextlib import ExitStack

F32 = mybir.dt.float32
BF16 = mybir.dt.bfloat16
AF = mybir.ActivationFunctionType
ALU = mybir.AluOpType

D = 1024
DFF = 2816
NJ = DFF // 128
SEQ = 4096
OWN = 2048
TT = 512
EPS = 1e-6
NRING = 6
NSTAT = 64


class Sched:
    def __init__(self):
        self.ops = []
        self.last_w = {}
        self.readers = {}

    def add(self, eng, fn, reads=(), writes=(), dma_key=None):
        idx = len(self.ops)
        deps = {}
        for r in reads:
            if r in self.last_w:
                deps[self.last_w[r]] = True
        for w in writes:
            if w in self.last_w:
                deps[self.last_w[w]] = True
            for rd in self.readers.get(w, ()):
                deps.setdefault(rd, False)
        deps.pop(idx, None)
        self.ops.append(dict(eng=eng, fn=fn, deps=deps, dma_key=dma_key, signal=False,
                             semval=None, dma_val=None))
        for w in writes:
            self.last_w[w] = idx
            self.readers[w] = []
        for r in reads:
            if r not in writes:
                self.readers.setdefault(r, []).append(idx)
        return idx

    def finalize(self):
        ops = self.ops
        for op in ops:
            need = []
            for d, hard in op["deps"].items():
                dop = ops[d]
                if dop["dma_key"] is None:
                    if dop["eng"] == op["eng"] and op["dma_key"] is None:
                        if op["eng"] == "pe":
                            continue
                    dop["signal"] = True
                need.append(d)
            op["need"] = need
        cnt = {}
        dcnt = {}
        for op in ops:
            if op["dma_key"] is not None:
                dcnt[op["dma_key"]] = dcnt.get(op["dma_key"], 0) + 16
                op["dma_val"] = dcnt[op["dma_key"]]
            elif op["signal"]:
                cnt[op["eng"]] = cnt.get(op["eng"], 0) + 1
                op["semval"] = cnt[op["eng"]]
        self.dma_keys = list(dcnt.keys())

    def emit(self, eng_name, eng, esems, dsems, final_wait=()):
        ops = self.ops
        known = {}

        def wait_for(d):
            dop = ops[d]
            if dop["dma_key"] is not None:
                sem, val, key = dsems[dop["dma_key"]], dop["dma_val"], ("d", dop["dma_key"])
            else:
                sem, val, key = esems[dop["eng"]], dop["semval"], ("e", dop["eng"])
            if known.get(key, 0) >= val:
                return
            eng.wait_ge(sem, val)
            known[key] = val

        for op in ops:
            if op["eng"] != eng_name:
                continue
            for d in sorted(op["need"]):
                wait_for(d)
            ins = op["fn"](eng)
            if op["dma_key"] is not None:
                ins.then_inc(dsems[op["dma_key"]], 16)
            elif op["signal"]:
                ins.then_inc(esems[eng_name], 1)
        for d in final_wait:
            wait_for(d)


def build_program(stage="full"):
    nc = bass.Bass("TRN2", target_bir_lowering=False)
    S = Sched()
    dram = {}

    def din(name, shape):
        dram[name] = nc.dram_tensor(name, list(shape), F32, kind="ExternalInput").ap()
        return dram[name]

    xin = din("xin", [SEQ, D])
    mem = din("mem", [256, D])
    rope = din("rope", [2, 128, SEQ])
    wg = [din("wg1", [D, DFF]), din("wg2", [D, DFF])]
    wu = [din("wu1", [D, DFF]), din("wu2", [D, DFF])]
    wdn = [din("wd1", [DFF, D]), din("wd2", [DFF, D])]
    w_kkv = din("w_kkv", [D, 512])
    w_q = din("w_q", [D, 1024])
    w_gu = din("w_gu", [D, 512])
    w_gv = din("w_gv", [D, 512])
    w_qm = din("w_qm", [D, 512])
    w_bg = din("w_bg", [D, 3072])
    w_pr = din("w_pr", [512, 3072])
    w_out = din("w_out", [D, D])
    w_mkv = din("w_mkv", [D, D])
    cols_d = din("cols", [128, 64])
    rows_d = din("rows", [1, 4096])
    wsT_d = din("wsT", [128, 512])
    out = nc.dram_tensor("out", [OWN, D], F32, kind="ExternalOutput").ap()
    x1s = nc.dram_tensor("x1s", [OWN, D], F32).ap()

    def kview(w):
        return w.rearrange("(k p) n -> p k n", p=128)

    with ExitStack() as es:
        def sb(name, shape, dt):
            return es.enter_context(nc.sbuf_tensor(name, list(shape), dt))

        def ps(name):
            return es.enter_context(nc.psum_tensor(name, [128, 2, 512], F32))

        xr = sb("xr", [128, NRING, D], F32)
        KT = sb("KT", [128, SEQ], BF16)
        VA = sb("VA", [128, 32, 128], BF16)
        VB = sb("VB", [128, 32, 128], BF16)
        wd = sb("wd", [128, NJ, D], BF16)
        NUNIT = 8
        wun = sb("wun", [128, NUNIT, 2048], BF16)
        hT = sb("hT", [128, 8, TT], BF16)
        AT = sb("AT", [128, NJ, TT], BF16)
        mx = sb("mx", [128, 3, 1024], BF16)
        vn = sb("vn", [128, 4, 512], BF16)
        htok = sb("htok", [128, 4, D], BF16)
        junk = sb("junk", [128, D], BF16)
        tbuf = [sb(f"tbuf{i}", [128, D], F32) for i in range(2)]
        sg = [sb(f"sg{i}", [128, 512], F32) for i in range(2)]
        cs = [sb("cs0", [128, 2, 512], F32)]
        gpost = [sb("gpost0", [128, D], F32)]
        gvn_b = sb("gvn_b", [128, 512], F32)
        bs_b = sb("bs_b", [128, 4, 128], F32)
        wsT = sb("wsT_sb", [128, 4, 128], BF16)
        KmT = sb("KmT", [128, 4, 256], BF16)
        Vm = sb("Vm", [128, 2, 512], BF16)
        cols = sb("cols_sb", [128, 64], F32)
        hb = sb("hb", [128, 24], F32)
        st = sb("st", [128, NSTAT], F32)
        sd = sb("sd", [128, NSTAT], F32)
        rs = sb("rs", [128, NSTAT], F32)
        ident = sb("ident", [128, 128], BF16)
        ones = sb("ones", [128, 128], BF16)
        bones = sb("bones", [128, 128], BF16)
        pa = [ps("pa0"), ps("pa1")]
        pb = [ps("pb0"), ps("pb1")]

        ws_slab = []
        WR = []
        mxf = [mx[:, i, :].bitcast(F32) for i in range(3)]
        QT = AT[:, 0:4, :]
        QmT = AT[:, 4:8, :]
        OT = AT[:, 8:12, :]
        OmT = AT[:, 12:16, :]
        gmT = AT[:, 16:20, :]
        sqb = AT[:, 20, :]

        state = dict(unit=0, stat=0, pa=0, pb=0, sgi=0, tb=0, csi=0, pi=0)

        def nxt(name, n):
            v = state[name]
            state[name] = (v + 1) % n
            return v

        def next_stat(n=1):
            v = state["stat"]
            if v + n > NSTAT:
                v = 0
            state["stat"] = v + n
            return v

        def dma(q, out_ap, in_ap, reads, writes, key):
            return S.add(q, lambda e: e.dma_start(out=out_ap, in_=in_ap), reads=reads, writes=writes,
                         dma_key=key)

        def alloc_units(n):
            u = state["unit"]
            if u + n > NUNIT:
                u = 0
            state["unit"] = (u + n) % NUNIT
            return u

        def load_slab(src_ap, view_fn=None):
            u = alloc_units(2)
            view = wun[:, u:u + 2, :].rearrange("p a n -> p (a n)").rearrange("p (k n) -> p k n", k=8)
            res = [f"wu{u}", f"wu{u + 1}"]
            dma("pool", view, src_ap, [], res, f"wu{u}")
            ws_slab.append(view)
            WR.append(res)
            return len(ws_slab) - 1

        def load_unit(src_ap, k):
            u = alloc_units(1)
            view = wun[:, u, :].rearrange("p (k n) -> p k n", k=k)
            res = [f"wu{u}"]
            dma("pool", view, src_ap, [], res, f"wu{u}")
            ws_slab.append(view)
            WR.append(res)
            return len(ws_slab) - 1

        def rstd_cols(c0, n, bias=EPS):
            S.add("act", lambda e: e.activation(out=sd[:, c0:c0 + n], in_=st[:, c0:c0 + n], func=AF.Ln,
                                                bias=bias, scale=1.0),
                  reads=[f"st{c}" for c in range(c0, c0 + n)], writes=[f"sd{c}" for c in range(c0, c0 + n)])
            S.add("act", lambda e: e.activation(out=rs[:, c0:c0 + n], in_=sd[:, c0:c0 + n], func=AF.Exp, scale=-0.5),
                  reads=[f"sd{c}" for c in range(c0, c0 + n)], writes=[f"rs{c}" for c in range(c0, c0 + n)])

        def norm_to_hT(xaps, xres, gcol0, nblk=4):
            c0 = next_stat(nblk)
            for b in range(nblk):
                S.add("act", lambda e, b=b: e.activation(out=junk[:, :], in_=xaps[b], func=AF.Square,
                                                         scale=1.0 / 32.0, accum_out=st[:, c0 + b:c0 + b + 1]),
                      reads=[xres[b]], writes=["junk", f"st{c0 + b}"])
            rstd_cols(c0, nblk)
            for b in range(nblk):
                S.add("dve", lambda e, b=b: e.tensor_scalar(out=htok[:, b, :], in0=xaps[b],
                                                            scalar1=rs[:, c0 + b:c0 + b + 1], scalar2=None,
                                                            op0=ALU.mult),
                      reads=[xres[b], f"rs{c0 + b}"], writes=[f"htok{b}"])
            for k in range(8):
                p = nxt("pb", 2)
                pt = pb[p][:, 0, :].bitcast(BF16)

                def tr(e, k=k, pt=pt):
                    for b in range(nblk):
                        i = e.transpose(out=pt[:, b * 128:(b + 1) * 128], in_=htok[:, b, k * 128:(k + 1) * 128],
                                        identity=ident[:, :])
                    return i
                S.add("pe", tr, reads=[f"htok{b}" for b in range(nblk)] + ["ident"], writes=[f"pb{p}"])
                S.add("dve", lambda e, k=k, pt=pt: e.tensor_scalar(
                    out=hT[:, k, 0:nblk * 128], in0=pt[:, 0:nblk * 128],
                    scalar1=cols[:, gcol0 + k:gcol0 + k + 1], scalar2=None, op0=ALU.mult),
                    reads=[f"pb{p}", "cols"], writes=[f"hT{k}"])

        HT_ALL = [f"hT{k}" for k in range(8)]
        HU_ALL = [f"hU{k}" for k in range(8)]
        _kmt_flat = KmT[:, :, :].rearrange("p a n -> p (a n)")

        def hu(k):
            if k < 4:
                return vn[:, k, :]
            if k < 6:
                return _kmt_flat[:, (k - 4) * 512:(k - 3) * 512]
            return Vm[:, k - 6, :]

        def norm_block(xap_b, xres_b, b, gcol0, dst_b=False, slot=None):
            norm_a(xap_b, xres_b, b, slot=slot)
            norm_b(b, gcol0, dst_b=dst_b, slot=slot)

        def norm_a(xap_b, xres_b, b, slot=None):
            sl = b if slot is None else slot
            c = next_stat(1)
            S.add("act", lambda e: e.activation(out=junk[:, :], in_=xap_b, func=AF.Square, scale=1.0 / 32.0,
                                                accum_out=st[:, c:c + 1]),
                  reads=[xres_b], writes=["junk", f"st{c}"])
            rstd_cols(c, 1)
            S.add("act", lambda e: e.activation(out=htok[:, sl, :], in_=xap_b, func=AF.Copy, scale=rs[:, c:c + 1]),
                  reads=[xres_b, f"rs{c}"], writes=[f"htok{sl}"])

        def norm_b(b, gcol0, deep=False, dst_b=False, slot=None):
            sl = b if slot is None else slot
            if deep:
                ptile, pname = [(pa[0], "pa0"), (pa[1], "pa1"), (pb[0], "pb0"), (pb[1], "pb1")][b % 4]
            else:
                p = nxt("pa", 2)
                ptile, pname = pa[p], f"pa{p}"
            pt = ptile[:, 0, :].bitcast(BF16)

            def tr(e):
                for k in range(8):
                    i = e.transpose(out=pt[:, k * 128:(k + 1) * 128], in_=htok[:, sl, k * 128:(k + 1) * 128],
                                    identity=ident[:, :])
                return i
            S.add("pe", tr, reads=[f"htok{sl}", "ident"], writes=[pname])
            if dst_b:
                ptk = pt.rearrange("p (k n) -> p k n", k=8)
                S.add("dve", lambda e: e.tensor_tensor(
                    out=vn[:, 0:4, b * 128:(b + 1) * 128], in0=ptk[:, 0:4, :],
                    in1=cols[:, gcol0:gcol0 + 4].unsqueeze(2).to_broadcast([128, 4, 128]), op=ALU.mult),
                    reads=[pname, "cols"], writes=HU_ALL[0:4])
                S.add("dve", lambda e: e.tensor_tensor(
                    out=_kmt_flat.rearrange("p (k n) -> p k n", k=2)[:, :, b * 128:(b + 1) * 128], in0=ptk[:, 4:6, :],
                    in1=cols[:, gcol0 + 4:gcol0 + 6].unsqueeze(2).to_broadcast([128, 2, 128]), op=ALU.mult),
                    reads=[pname, "cols"], writes=HU_ALL[4:6])
                S.add("dve", lambda e: e.tensor_tensor(
                    out=Vm[:, 0:2, b * 128:(b + 1) * 128], in0=ptk[:, 6:8, :],
                    in1=cols[:, gcol0 + 6:gcol0 + 8].unsqueeze(2).to_broadcast([128, 2, 128]), op=ALU.mult),
                    reads=[pname, "cols"], writes=HU_ALL[6:8])
                return
            S.add("dve", lambda e: e.tensor_tensor(
                out=hT[:, :, b * 128:(b + 1) * 128], in0=pt.rearrange("p (k n) -> p k n", k=8),
                in1=cols[:, gcol0:gcol0 + 8].unsqueeze(2).to_broadcast([128, 8, 128]), op=ALU.mult),
                reads=[pname, "cols"], writes=HT_ALL)

        def post_norm_residual(yps, yres, xap, xres, gp, gres):
            c = next_stat(1)
            yv = yps[:, :, :].rearrange("p a n -> p (a n)")
            S.add("act", lambda e: e.activation(out=junk[:, :], in_=yv, func=AF.Square, scale=1.0 / 32.0,
                                                accum_out=st[:, c:c + 1]),
                  reads=[yres], writes=["junk", f"st{c}"])
            rstd_cols(c, 1)
            t = nxt("tb", 2)
            S.add("dve", lambda e: e.scalar_tensor_tensor(out=tbuf[t][:, :], in0=yv, scalar=rs[:, c:c + 1],
                                                          in1=gp[:, :], op0=ALU.mult, op1=ALU.mult),
                  reads=[yres, f"rs{c}", gres], writes=[f"tbuf{t}"])
            S.add("dve", lambda e: e.tensor_tensor(out=xap, in0=xap, in1=tbuf[t][:, :], op=ALU.add),
                  reads=[xres, f"tbuf{t}"], writes=[xres])

        def pipeline3(n, pe_fn, post_fn, norm_fn, skew=2):
            for i in range(n + skew):
                if i < n:
                    pe_fn(i)
                if 0 <= i - 1 < n:
                    post_fn(i - 1)
                if 0 <= i - skew < n and norm_fn is not None:
                    norm_fn(i - skew)

        def ffn_tile(which, xaps, xres, gp, gres, post_hook=None, norm_fn=None, up_hook=None, pre_dn_hook=None):
            wgv, wuv = kview(wg[which]), kview(wu[which])
            for jj in range(NJ // 2):
                if jj == 2 and up_hook is not None:
                    up_hook()
                if wd_pending and wd_pending[0] == which:
                    for piece in (2 * jj, 2 * jj + 1):
                        if piece < NJ // 2:
                            load_wd_piece(which, piece)
                    if jj == NJ // 2 - 1:
                        wd_pending.clear()
                lg = load_unit(wgv[:, :, jj * 256:(jj + 1) * 256], 8)
                lu = load_unit(wuv[:, :, jj * 256:(jj + 1) * 256], 8)
                for c in range(2):
                    j = jj * 2 + c
                    p = nxt("pa", 2)

                    def mm(e, lg=lg, lu=lu, c=c, p=p):
                        for a, l in ((0, lg), (1, lu)):
                            for k in range(8):
                                i = e.matmul(pa[p][:, a, :], lhsT=ws_slab[l][:, k, c * 128:(c + 1) * 128],
                                             rhs=hT[:, k, :], start=(k == 0), stop=(k == 7))
                        return i
                    S.add("pe", mm, reads=WR[lg] + WR[lu] + HT_ALL, writes=[f"pa{p}"])
                    g = nxt("sgi", 2)
                    S.add("act", lambda e, p=p, g=g: e.activation(out=sg[g][:, :], in_=pa[p][:, 0, :], func=AF.Silu),
                          reads=[f"pa{p}"], writes=[f"sg{g}"])
                    S.add("dve", lambda e, p=p, g=g, j=j: e.tensor_tensor(out=AT[:, j, :], in0=sg[g][:, :],
                                                                         in1=pa[p][:, 1, :], op=ALU.mult),
                          reads=[f"sg{g}", f"pa{p}"], writes=[f"at{j}"])
            pbs = {}

            def emit_dn(b):
                p = nxt("pb", 2)
                pbs[b] = p

                def dn(e, b=b, p=p):
                    for a in range(2):
                        for j in range(NJ):
                            i = e.matmul(pb[p][:, a, :], lhsT=AT[:, j, b * 128:(b + 1) * 128],
                                         rhs=wd[:, j, a * 512:(a + 1) * 512], start=(j == 0), stop=(j == NJ - 1))
                    return i
                S.add("pe", dn, reads=[f"at{j}" for j in range(NJ)] + WD_ALL, writes=[f"pb{p}"])

            def finish(b):
                p = pbs[b]
                post_norm_residual(pb[p], f"pb{p}", xaps[b], xres[b], gp, gres)
                if post_hook is not None:
                    post_hook(b)
            if pre_dn_hook is not None:
                pre_dn_hook()
            pipeline3(4, emit_dn, finish, norm_fn, skew=1)

        WD_ALL = [f"wd{i}" for i in range(NJ // 2)]

        def load_wd_piece(which, i):
            v = wdn[which].rearrange("(j p) n -> p j n", p=128)
            dma("pool", wd[:, 2 * i:2 * i + 2, :], v[:, 2 * i:2 * i + 2, :], [], [f"wd{i}"], f"wd{i}")

        def load_row_table(dst, dres, off, n, key):
            dma("sp", dst, rows_d[:, off:off + n].partition_broadcast(128), [], [dres], key)

        def rope_norm(pq, pres, gc, gpc, csb, csres, out_ap, out_res, split=None):
            S.add("act", lambda e: e.activation(out=sqb, in_=pq[:, 0, :], func=AF.Square, scale=0.125),
                  reads=[pres], writes=["at20"])
            p2 = nxt("pb", 2)
            S.add("pe", lambda e: e.matmul(pb[p2][:, 0, :], lhsT=bones[:, :], rhs=sqb, start=True, stop=True),
                  reads=["at20", "bones"], writes=[f"pb{p2}"])
            S.add("act", lambda e: e.activation(out=mxf[2], in_=pb[p2][:, 0, :], func=AF.Ln, bias=EPS, scale=1.0),
                  reads=[f"pb{p2}"], writes=["mx2"])
            S.add("act", lambda e: e.activation(out=mxf[2], in_=mxf[2], func=AF.Exp, scale=-0.5),
                  reads=["mx2"], writes=["mx2"])
            S.add("dve", lambda e: e.scalar_tensor_tensor(out=mxf[0], in0=pq[:, 0, :], scalar=cols[:, gc:gc + 1],
                                                          in1=csb[:, 0, :], op0=ALU.mult, op1=ALU.mult),
                  reads=[pres, csres, "cols", "at20"], writes=["mx0"])
            S.add("dve", lambda e: e.scalar_tensor_tensor(out=mxf[1], in0=pq[:, 1, :], scalar=cols[:, gpc:gpc + 1],
                                                          in1=csb[:, 1, :], op0=ALU.mult, op1=ALU.mult),
                  reads=[pres, csres, "cols"], writes=["mx1"])
            S.add("dve", lambda e: e.tensor_tensor(out=mxf[0], in0=mxf[0], in1=mxf[1], op=ALU.add),
                  reads=["mx0", "mx1"], writes=["mx0"])
            if split is None:
                S.add("dve", lambda e: e.tensor_tensor(out=out_ap, in0=mxf[0], in1=mxf[2], op=ALU.mult),
                      reads=["mx0", "mx2"], writes=[out_res])
            else:
                (oa, ra), (ob, rb) = split
                S.add("dve", lambda e: e.tensor_tensor(out=oa, in0=mxf[0][0:64, :], in1=mxf[2][0:64, :], op=ALU.mult),
                      reads=["mx0", "mx2"], writes=[ra])
                S.add("dve", lambda e: e.tensor_tensor(out=ob, in0=mxf[0][64:128, :], in1=mxf[2][64:128, :],
                                                       op=ALU.mult),
                      reads=["mx0", "mx2"], writes=[rb])

        def load_rope_tile(tok0):
            c = nxt("csi", 1)
            dma("sp", cs[c][:, :, :], rope[:, :, tok0:tok0 + TT].rearrange("t p n -> p t n"), [], [f"cs{c}"],
                f"cs{c}")
            return c

        S.add("pool", lambda e: e.memset(ident[:, :], 1.0), writes=["ident"])
        S.add("pool", lambda e: e.affine_select(out=ident[:, :], in_=ident[:, :], pattern=[[-1, 128]],
                                                compare_op=ALU.is_equal, fill=0.0, base=0, channel_multiplier=1),
              reads=["ident"], writes=["ident"])
        S.add("pool", lambda e: e.memset(ones[:, :], 1.0), writes=["ones"])
        S.add("pool", lambda e: e.memset(bones[:, :], 0.0), writes=["bones"])
        S.add("pool", lambda e: e.memset(bones[0:64, 0:64], 1.0), reads=["bones"], writes=["bones"])
        S.add("pool", lambda e: e.memset(bones[64:128, 64:128], 1.0), reads=["bones"], writes=["bones"])
        S.add("dve", lambda e: e.memset(st[:, :], 0.0), writes=[f"st{c}" for c in range(NSTAT)])
        S.add("dve", lambda e: e.memset(VA[:, :, 64:128], 1.0), writes=[f"VA{t}" for t in range(SEQ // TT)])
        S.add("dve", lambda e: e.memset(VB[:, :, 0:64], 1.0), writes=[f"VB{t}" for t in range(SEQ // TT)])
        dma("sp", cols[:, :], cols_d[:, :], [], ["cols"], "cols")
        load_row_table(gvn_b[:, :], "gvn_b", 3072, 512, "gvn_b")
        load_row_table(bs_b[:, :, :].rearrange("p g n -> p (g n)"), "bs_b", 3584, 512, "bs_b")
        dma("pool", wsT[:, :, :].rearrange("p g n -> p (g n)"), wsT_d[:, :], [], ["wsT"], "wsT")
        S.add("dve", lambda e: e.tensor_scalar(out=hb[:, :], in0=cols[:, 32:56], scalar1=0.5, scalar2=None,
                                               op0=ALU.mult), reads=["cols"], writes=["hb"])

        def ring_slot(gblk):
            return gblk % NRING

        def xap(gblk):
            return xr[:, ring_slot(gblk), :]

        def xres(gblk):
            return f"xr{ring_slot(gblk)}"

        xsrc = {}

        def load_block(gblk):
            src, rd = xsrc[gblk]
            s = ring_slot(gblk)
            dma("sp", xr[:, s, :], src, rd, [f"xr{s}"], f"xr{s}")

        nA = SEQ // TT if stage == "full" else (1 if stage == "A1" else OWN // TT)
        nB = OWN // TT if stage == "full" else 0
        gb = 0
        tilesA, tilesB = [], []
        for t in range(nA):
            blks = []
            for b in range(4):
                r0 = t * TT + b * 128
                xsrc[gb] = (xin[r0:r0 + 128, :], [])
                blks.append(gb)
                gb += 1
            tilesA.append(blks)
        for t in range(nB):
            blks = []
            for b in range(4):
                r0 = t * TT + b * 128
                xsrc[gb] = (x1s[r0:r0 + 128, :], [f"x1s{t * 4 + b}"])
                blks.append(gb)
                gb += 1
            tilesB.append(blks)
        all_tiles = tilesA + tilesB
        loaded = set()

        def refill(ti, b):
            tgt = (ti + 1, b + 2) if b < 2 else (ti + 2, b - 2)
            if tgt[0] < len(all_tiles):
                g = all_tiles[tgt[0]][tgt[1]]
                if g not in loaded:
                    load_block(g)
                    loaded.add(g)

        def ensure_loaded(ti):
            for g in all_tiles[ti]:
                if g not in loaded:
                    load_block(g)
                    loaded.add(g)
            if ti + 1 < len(all_tiles):
                for g in all_tiles[ti + 1][:2]:
                    if g not in loaded:
                        load_block(g)
                        loaded.add(g)

        def mem_kv():
            for b in range(2):
                dma("sp", tbuf[b][:, :], mem[b * 128:(b + 1) * 128, :], [], [f"tbuf{b}"], f"tbuf{b}")
            for b in range(2):
                norm_a(tbuf[b][:, :], f"tbuf{b}", b)
            for b in range(2):
                norm_b(b, 24)
            s0 = load_slab(kview(w_mkv)[:, :, 0:512])
            s1 = load_slab(kview(w_mkv)[:, :, 512:1024])
            for m in range(4):
                p = nxt("pa", 2)

                def mmk(e, m=m, p=p):
                    for k in range(8):
                        i = e.matmul(pa[p][:, 0, 0:256], lhsT=ws_slab[s0][:, k, m * 128:(m + 1) * 128],
                                     rhs=hT[:, k, 0:256], start=(k == 0), stop=(k == 7))
                    return i
                S.add("pe", mmk, reads=[*WR[s0]] + HT_ALL, writes=[f"pa{p}"])
                S.add("dve", lambda e, m=m, p=p: e.tensor_copy(out=KmT[:, m, :], in_=pa[p][:, 0, 0:256]),
                      reads=[f"pa{p}"], writes=["KmT"])
            for b in range(2):
                p = nxt("pb", 2)

                def mmv(e, b=b, p=p):
                    for k in range(8):
                        i = e.matmul(pb[p][:, 0, :], lhsT=hT[:, k, b * 128:(b + 1) * 128], rhs=ws_slab[s1][:, k, :],
                                     start=(k == 0), stop=(k == 7))
                    return i
                S.add("pe", mmv, reads=[*WR[s1]] + HT_ALL, writes=[f"pb{p}"])
                S.add("dve", lambda e, b=b, p=p: e.tensor_copy(out=Vm[:, b, :], in_=pb[p][:, 0, :]),
                      reads=[f"pb{p}"], writes=["Vm"])


        tile_gcol = [0] * len(tilesA) + [8] * len(tilesB)
        done_a, done_b = set(), set()

        def first_norm(ti, only_a=False, blocks=range(4)):
            for b in blocks:
                if (ti, b) not in done_a:
                    g = all_tiles[ti][b]
                    if g not in loaded:
                        load_block(g)
                        loaded.add(g)
                    norm_a(xap(g), xres(g), b)
                    done_a.add((ti, b))
            if only_a:
                return
            for b in blocks:
                if (ti, b) not in done_b:
                    norm_b(b, tile_gcol[ti], deep=(len(list(blocks)) == 4))
                    done_b.add((ti, b))

        wd_pending = [0]
        def load_gpost(off, half):
            load_row_table(gpost[0][:, :], "gpost0", off, 1024, "gpost0")
            if half:
                S.add("dve", lambda e: e.tensor_scalar(out=gpost[0][:, :], in0=gpost[0][:, :], scalar1=0.5,
                                                       scalar2=None, op0=ALU.mult),
                      reads=["gpost0"], writes=["gpost0"])
        load_gpost(0, True)
        pending_kv = []

        def kv_stage_body(ti):
            c = load_rope_tile(ti * TT)
            s = load_slab(kview(w_kkv))
            p = nxt("pa", 2)

            def mmk2(e, s=s, p=p):
                for a in range(2):
                    for k in range(8):
                        i = e.matmul(pa[p][:, a, :], lhsT=ws_slab[s][:, k, a * 128:(a + 1) * 128], rhs=hu(k),
                                     start=(k == 0), stop=(k == 7))
                return i
            S.add("pe", mmk2, reads=[*WR[s]] + HU_ALL, writes=[f"pa{p}"])
            pk = p
            p = nxt("pb", 2)

            def mmv2(e, s=s, p=p):
                for b in range(4):
                    for k in range(8):
                        i = e.matmul(pb[p][:, 0, b * 128:(b + 1) * 128], lhsT=hu(k)[:, b * 128:(b + 1) * 128],
                                     rhs=ws_slab[s][:, k, 256:384], start=(k == 0), stop=(k == 7))
                return i
            S.add("pe", mmv2, reads=[*WR[s]] + HU_ALL, writes=[f"pb{p}"])
            S.add("dve", lambda e, p=p, ti=ti: e.tensor_copy(
                out=VA[:, ti * 4:(ti + 1) * 4, 0:64],
                in_=pb[p][:, 0, :].rearrange("p (b n) -> p b n", b=4)[:, :, 0:64]),
                reads=[f"pb{p}"], writes=[f"VA{ti}"])
            S.add("dve", lambda e, p=p, ti=ti: e.tensor_copy(
                out=VB[:, ti * 4:(ti + 1) * 4, 64:128],
                in_=pb[p][:, 0, :].rearrange("p (b n) -> p b n", b=4)[:, :, 64:128]),
                reads=[f"pb{p}"], writes=[f"VB{ti}"])
            rope_norm(pa[pk], f"pa{pk}", 58, 59, cs[c], f"cs{c}", KT[:, ti * TT:(ti + 1) * TT], f"KT{ti}")

        store_ops = []
        for ti, blks in enumerate(tilesA):
            own = ti < OWN // TT
            ensure_loaded(ti)
            xa = [xap(g) for g in blks]
            xs = [xres(g) for g in blks]
            first_norm(ti)

            def post_a(b, ti=ti, own=own, xa=xa, xs=xs):
                if own:
                    r0 = ti * TT + b * 128
                    dst = x1s if stage == "full" else out
                    o = dma("sp", dst[r0:r0 + 128, :], xa[b], [xs[b]], [f"x1s{ti * 4 + b}", f"stq{b}"], f"x1st{b}")
                    store_ops.append(o)
                if stage != "full":
                    refill(ti, b)

            has_next = stage == "full" and ti + 1 < len(tilesA)

            def next_n1(b, slot, ti=ti):
                g = all_tiles[ti + 1][b]
                if g not in loaded:
                    load_block(g)
                    loaded.add(g)
                norm_a(xap(g), xres(g), b, slot=slot)
                norm_b(b, 0, slot=slot)
                done_a.add((ti + 1, b))
                done_b.add((ti + 1, b))

            def pre_dn(has_next=has_next):
                if has_next:
                    next_n1(0, 0)
                    next_n1(1, 1)

            def norm_a2(b, ti=ti, xa=xa, xs=xs, has_next=has_next):
                if b < 3 or not has_next:
                    norm_block(xa[b], xs[b], b, 8, dst_b=True)
                else:
                    norm_a(xa[b], xs[b], b)
                refill(ti, b)
                if b == 2 and has_next:
                    next_n1(2, 0)
                    next_n1(3, 1)
            ffn_tile(0, xa, xs, gpost[0], "gpost0", post_hook=post_a, norm_fn=norm_a2 if stage == "full" else None,
                     up_hook=pending_kv.pop() if pending_kv else None, pre_dn_hook=pre_dn)
            if stage != "full":
                continue

            def kv_stage(ti=ti, deferred=has_next):
                if deferred:
                    norm_b(3, 8, dst_b=True)
                kv_stage_body(ti)
            if has_next:
                pending_kv.append(kv_stage)
            else:
                kv_stage()

        if stage == "full":
            S.add("dve", lambda e: e.memset(sd[:, 0:1], 0.0), reads=HU_ALL,
                  writes=["vn0", "vn1", "vn2", "vn3", "KmT", "Vm", "sd0"])
            mem_kv()
            wd_pending.append(1)
            load_gpost(1024, False)
            KT_ALL = [f"KT{t}" for t in range(SEQ // TT)]
            V_ALL = [f"VA{t}" for t in range(SEQ // TT)] + [f"VB{t}" for t in range(SEQ // TT)]
            SC_A = 0.125
            SC_M = 128.0 ** -0.5
            for tb, blks in enumerate(tilesB):
                ti = len(tilesA) + tb
                ensure_loaded(ti)
                xa = [xap(g) for g in blks]
                xs = [xres(g) for g in blks]
                first_norm(ti)
                c = load_rope_tile(tb * TT)
                sv = load_slab(kview(w_gv))
                c0 = next_stat(4)
                for b in range(4):
                    p = nxt("pb", 2)

                    def mmgv(e, b=b, p=p, sv=sv):
                        for k in range(8):
                            i = e.matmul(pb[p][:, 0, :], lhsT=hT[:, k, b * 128:(b + 1) * 128], rhs=ws_slab[sv][:, k, :],
                                         start=(k == 0), stop=(k == 7))
                        return i
                    S.add("pe", mmgv, reads=[*WR[sv]] + HT_ALL, writes=[f"pb{p}"])
                    gbuf = tbuf[b // 2][:, (b % 2) * 512:(b % 2 + 1) * 512]
                    S.add("act", lambda e, p=p, gbuf=gbuf: e.activation(out=gbuf, in_=pb[p][:, 0, :],
                                                                        func=AF.Gelu_apprx_tanh),
                          reads=[f"pb{p}"], writes=[f"tbuf{b // 2}"])
                    S.add("act", lambda e, gbuf=gbuf, b=b, c0=c0: e.activation(
                        out=junk[:, 0:512], in_=gbuf, func=AF.Square, scale=512.0 ** -0.5,
                        accum_out=st[:, c0 + b:c0 + b + 1]),
                        reads=[f"tbuf{b // 2}"], writes=["junk", f"st{c0 + b}"])
                rstd_cols(c0, 4)
                for b in range(4):
                    gbuf = tbuf[b // 2][:, (b % 2) * 512:(b % 2 + 1) * 512]
                    S.add("dve", lambda e, gbuf=gbuf, b=b, c0=c0: e.scalar_tensor_tensor(
                        out=vn[:, b, :], in0=gbuf, scalar=rs[:, c0 + b:c0 + b + 1], in1=gvn_b[:, :],
                        op0=ALU.mult, op1=ALU.mult),
                        reads=[f"tbuf{b // 2}", f"rs{c0 + b}", "gvn_b"], writes=[f"vn{b}"])
                sm = load_slab(kview(w_qm))
                for m in range(4):
                    p = nxt("pa", 2)

                    def mmqm(e, m=m, p=p, sm=sm):
                        for k in range(8):
                            i = e.matmul(pa[p][:, 0, :], lhsT=ws_slab[sm][:, k, m * 128:(m + 1) * 128],
                                         rhs=hT[:, k, :], start=(k == 0), stop=(k == 7))
                        return i
                    S.add("pe", mmqm, reads=[*WR[sm]] + HT_ALL, writes=[f"pa{p}"])
                    S.add("dve", lambda e, m=m, p=p: e.tensor_copy(out=QmT[:, m, :], in_=pa[p][:, 0, :]),
                          reads=[f"pa{p}"], writes=[f"at{4 + m}"])
                su = load_slab(kview(w_gu))
                for ch in range(4):
                    p = nxt("pa", 2)

                    def mmgu(e, ch=ch, p=p, su=su):
                        for k in range(8):
                            i = e.matmul(pa[p][:, 0, :], lhsT=ws_slab[su][:, k, ch * 128:(ch + 1) * 128],
                                         rhs=hT[:, k, :], start=(k == 0), stop=(k == 7))
                        for b in range(4):
                            i = e.matmul(pa[p][:, 1, b * 128:(b + 1) * 128], lhsT=vn[:, b, ch * 128:(ch + 1) * 128],
                                         rhs=wsT[:, ch, :], start=True, stop=True)
                        return i
                    S.add("pe", mmgu, reads=[*WR[su], "wsT"] + HT_ALL + [f"vn{b}" for b in range(4)],
                          writes=[f"pa{p}"])
                    g = nxt("sgi", 2)
                    S.add("act", lambda e, p=p, g=g: e.activation(out=sg[g][:, :], in_=pa[p][:, 0, :],
                                                                  func=AF.Gelu_apprx_tanh),
                          reads=[f"pa{p}"], writes=[f"sg{g}"])
                    t = nxt("tb", 2)
                    for b in range(4):
                        S.add("dve", lambda e, p=p, t=t, b=b, ch=ch: e.tensor_tensor(
                            out=tbuf[t][:, b * 128:(b + 1) * 128], in0=pa[p][:, 1, b * 128:(b + 1) * 128],
                            in1=bs_b[:, ch, :], op=ALU.add),
                            reads=[f"pa{p}", "bs_b"], writes=[f"tbuf{t}"])
                    S.add("dve", lambda e, g=g, t=t, ch=ch: e.scalar_tensor_tensor(
                        out=gmT[:, ch, :], in0=sg[g][:, :], scalar=0.5, in1=tbuf[t][:, 0:512], op0=ALU.mult,
                        op1=ALU.mult), reads=[f"sg{g}", f"tbuf{t}"], writes=[f"at{16 + ch}"])
                S.add("dve", lambda e: e.memset(AT[64:128, 0:4, :], 0.0), writes=[f"at{k}" for k in range(4)])
                S.add("dve", lambda e: e.memset(vn[0:64, :, :], 0.0), writes=[f"vn{k}" for k in range(4)])
                sq0 = load_slab(kview(w_q)[:, :, 0:512])
                sq1 = load_slab(kview(w_q)[:, :, 512:1024])
                qps = []
                for ch in range(4):
                    p = nxt("pa", 2)

                    def mmq(e, ch=ch, p=p, sq0=sq0, sq1=sq1):
                        for a, sl in ((0, sq0), (1, sq1)):
                            for k in range(8):
                                i = e.matmul(pa[p][:, a, :], lhsT=ws_slab[sl][:, k, ch * 128:(ch + 1) * 128],
                                             rhs=hT[:, k, :], start=(k == 0), stop=(k == 7))
                        return i
                    S.add("pe", mmq, reads=[*WR[sq0], *WR[sq1]] + HT_ALL, writes=[f"pa{p}"])
                    qps.append(p)
                    if ch >= 1:
                        pp = qps[ch - 1]
                        rope_norm(pa[pp], f"pa{pp}", 56, 57, cs[c], f"cs{c}", None, None,
                                  split=((AT[0:64, ch - 1, :], f"at{ch - 1}"), (vn[64:128, ch - 1, :], f"vn{ch - 1}")))
                pp = qps[3]
                rope_norm(pa[pp], f"pa{pp}", 56, 57, cs[c], f"cs{c}", None, None,
                          split=((AT[0:64, 3, :], "at3"), (vn[64:128, 3, :], "vn3")))
                for hp in range(4):
                    po = nxt("pb", 2)

                    sring = [(pa[0], "pa0"), (pa[1], "pa1"), (pb[1 - po], f"pb{1 - po}")]

                    def qk(kb, hp=hp, sring=sring):
                        pst, psn = sring[kb % 3]

                        def f(e, kb=kb, pst=pst):
                            e.matmul(pst[:, 0, :], lhsT=KT[:, kb * 128:(kb + 1) * 128], rhs=AT[:, hp, :],
                                     start=True, stop=True)
                            return e.matmul(pst[:, 1, :], lhsT=KT[:, kb * 128:(kb + 1) * 128],
                                            rhs=vn[:, hp, :], start=True, stop=True)
                        S.add("pe", f, reads=KT_ALL + [f"at{hp}", f"vn{hp}"], writes=[psn])
                        pi = nxt("pi", 3)
                        S.add("act", lambda e, pst=pst, pi=pi: e.activation(
                            out=mx[:, pi, :], in_=pst[:, :, :].rearrange("p a n -> p (a n)"), func=AF.Exp,
                            scale=SC_A), reads=[psn], writes=[f"mx{pi}"])
                        return pi

                    def pv(kb, pi, po=po):
                        def f(e, kb=kb, pi=pi):
                            st_, sp_ = (kb == 0), (kb == 31)
                            e.matmul(pb[po][:, 0, :], lhsT=VA[:, kb, :], rhs=mx[:, pi, 0:512], start=st_, stop=sp_)
                            return e.matmul(pb[po][:, 1, :], lhsT=VB[:, kb, :], rhs=mx[:, pi, 512:1024], start=st_,
                                            stop=sp_)
                        S.add("pe", f, reads=V_ALL + [f"mx{pi}"], writes=[f"pb{po}"])

                    pis = {}
                    for kb in range(32 + 2):
                        if kb < 32:
                            pis[kb] = qk(kb)
                        if kb - 2 >= 0:
                            pv(kb - 2, pis[kb - 2])
                    g = nxt("sgi", 2)
                    t = nxt("tb", 2)
                    S.add("dve", lambda e, po=po, t=t: e.tensor_copy(
                        out=tbuf[t][:, :], in_=pb[po][:, :, :].rearrange("p a n -> p (a n)")),
                        reads=[f"pb{po}"], writes=[f"tbuf{t}"])
                    dma("sp", sg[g][0:64, :], tbuf[t][64:128, 0:512], [f"tbuf{t}"], [f"sg{g}"], f"sgd{g}")
                    dma("sp", sg[g][64:128, :], tbuf[t][0:64, 512:1024], [f"tbuf{t}"], [f"sg{g}", f"sg{g}x"],
                        f"sgd{g}x")
                    S.add("dve", lambda e, g=g: e.reciprocal(out=sg[g][:, :], in_=sg[g][:, :]),
                          reads=[f"sg{g}", f"sg{g}x"], writes=[f"sg{g}"])
                    S.add("dve", lambda e, t=t, g=g, hp=hp: e.scalar_tensor_tensor(
                        out=OT[0:64, hp, :], in0=tbuf[t][0:64, 0:512], scalar=0.5, in1=sg[g][0:64, :], op0=ALU.mult,
                        op1=ALU.mult), reads=[f"tbuf{t}", f"sg{g}"], writes=[f"at{8 + hp}"])
                    S.add("dve", lambda e, t=t, g=g, hp=hp: e.scalar_tensor_tensor(
                        out=OT[64:128, hp, :], in0=tbuf[t][64:128, 512:1024], scalar=0.5, in1=sg[g][64:128, :],
                        op0=ALU.mult, op1=ALU.mult), reads=[f"tbuf{t}", f"sg{g}"], writes=[f"at{8 + hp}"])
                mem_state = {}

                def mem_s(m):
                    p = nxt("pa", 2)

                    def mms(e, m=m, p=p):
                        e.matmul(pa[p][:, 0, :], lhsT=KmT[:, m, 0:128], rhs=QmT[:, m, :], start=True, stop=True)
                        return e.matmul(pa[p][:, 1, :], lhsT=KmT[:, m, 128:256], rhs=QmT[:, m, :], start=True, stop=True)
                    S.add("pe", mms, reads=["KmT", f"at{4 + m}"], writes=[f"pa{p}"])
                    pi = nxt("pi", 3)
                    S.add("act", lambda e, p=p, pi=pi: e.activation(
                        out=mx[:, pi, :], in_=pa[p][:, :, :].rearrange("p a n -> p (a n)"), func=AF.Exp, scale=SC_M),
                        reads=[f"pa{p}"], writes=[f"mx{pi}"])
                    mem_state[m] = pi

                def mem_o(m):
                    pi = mem_state[m]
                    po = nxt("pb", 2)

                    def mmo(e, m=m, pi=pi, po=po):
                        e.matmul(pb[po][:, 0, :], lhsT=Vm[:, 0, m * 128:(m + 1) * 128], rhs=mx[:, pi, 0:512],
                                 start=True, stop=False)
                        e.matmul(pb[po][:, 0, :], lhsT=Vm[:, 1, m * 128:(m + 1) * 128], rhs=mx[:, pi, 512:1024],
                                 start=False, stop=True)
                        e.matmul(pb[po][:, 1, :], lhsT=ones[:, :], rhs=mx[:, pi, 0:512], start=True, stop=False)
                        return e.matmul(pb[po][:, 1, :], lhsT=ones[:, :], rhs=mx[:, pi, 512:1024], start=False,
                                        stop=True)
                    S.add("pe", mmo, reads=["Vm", f"mx{pi}", "ones"], writes=[f"pb{po}"])
                    g = nxt("sgi", 2)
                    S.add("act", lambda e, po=po, g=g: e.activation(out=sg[g][:, :], in_=pb[po][:, 1, :], func=AF.Ln),
                          reads=[f"pb{po}"], writes=[f"sg{g}"])
                    S.add("act", lambda e, g=g: e.activation(out=sg[g][:, :], in_=sg[g][:, :], func=AF.Exp, scale=-1.0),
                          reads=[f"sg{g}"], writes=[f"sg{g}"])
                    S.add("dve", lambda e, po=po, g=g, m=m: e.scalar_tensor_tensor(
                        out=OmT[:, m, :], in0=pb[po][:, 0, :], scalar=0.5, in1=sg[g][:, :], op0=ALU.mult,
                        op1=ALU.mult), reads=[f"pb{po}", f"sg{g}"], writes=[f"at{12 + m}"])
                mem_s(0)
                for m in range(1, 4):
                    mem_s(m)
                    mem_o(m - 1)
                mem_o(3)
                brT = [OT, gmT, OmT]
                brres = [[f"at{8 + k}" for k in range(4)], [f"at{16 + k}" for k in range(4)],
                         [f"at{12 + k}" for k in range(4)]]
                gslot = None
                pslot = None
                macc = mxf[0]
                mtmp = mxf[1]
                for cc in range(8):
                    for r in range(3):
                        q = cc * 3 + r
                        if q % 4 == 0:
                            gslot = load_slab(kview(w_bg)[:, :, (q // 4) * 512:(q // 4 + 1) * 512])
                            pslot = load_unit(w_pr.rearrange("(k p) n -> p k n", p=128)[:, :, (q // 4) * 512:
                                                                                        (q // 4 + 1) * 512], 4)
                        pt4, pres = [(pa[0], "pa0"), (pa[1], "pa1"), (pb[0], "pb0"), (pb[1], "pb1")][q % 4]

                        def mmg(e, q=q, r=r, pt4=pt4, gslot=gslot, pslot=pslot):
                            for k in range(8):
                                e.matmul(pt4[:, 0, :], lhsT=ws_slab[gslot][:, k, (q % 4) * 128:(q % 4 + 1) * 128],
                                         rhs=hT[:, k, :], start=(k == 0), stop=(k == 7))
                            for k in range(4):
                                i = e.matmul(pt4[:, 1, :], lhsT=ws_slab[pslot][:, k, (q % 4) * 128:(q % 4 + 1) * 128],
                                             rhs=brT[r][:, k, :], start=(k == 0), stop=(k == 3))
                            return i
                        S.add("pe", mmg, reads=[*WR[gslot], *WR[pslot]] + HT_ALL + brres[r], writes=[pres])
                        g = nxt("sgi", 2)
                        S.add("act", lambda e, pt4=pt4, g=g, q=q: e.activation(out=sg[g][:, :], in_=pt4[:, 0, :],
                                                                               func=AF.Tanh, bias=hb[:, q:q + 1],
                                                                               scale=0.5),
                              reads=[pres, "hb"], writes=[f"sg{g}"])
                        dst = macc if r == 0 else mtmp
                        dres = "mx0" if r == 0 else "mx1"
                        S.add("dve", lambda e, pt4=pt4, g=g, dst=dst: e.scalar_tensor_tensor(
                            out=dst, in0=sg[g][:, :], scalar=1.0, in1=pt4[:, 1, :], op0=ALU.add, op1=ALU.mult),
                            reads=[f"sg{g}", pres], writes=[dres])
                        if r == 1:
                            S.add("dve", lambda e: e.tensor_tensor(out=macc, in0=macc, in1=mtmp, op=ALU.add),
                                  reads=["mx0", "mx1"], writes=["mx0"])
                        if r == 2:
                            S.add("dve", lambda e, cc=cc: e.tensor_tensor(out=AT[:, cc, :], in0=macc, in1=mtmp,
                                                                         op=ALU.add),
                                  reads=["mx0", "mx1"], writes=[f"at{cc}"])
                so0 = load_slab(kview(w_out)[:, :, 0:512])
                so1 = load_slab(kview(w_out)[:, :, 512:1024])
                wps = {}

                def emit_wo(b):
                    p = nxt("pb", 2)
                    wps[b] = p

                    def mmo2(e, b=b, p=p, so0=so0, so1=so1):
                        for a, sl in ((0, so0), (1, so1)):
                            for k in range(8):
                                i = e.matmul(pb[p][:, a, :], lhsT=AT[:, k, b * 128:(b + 1) * 128], rhs=ws_slab[sl][:, k, :],
                                             start=(k == 0), stop=(k == 7))
                        return i
                    S.add("pe", mmo2, reads=[*WR[so0], *WR[so1]] + [f"at{k}" for k in range(8)],
                          writes=[f"pb{p}"])

                def post_wo(b):
                    p = wps[b]
                    post_norm_residual(pb[p], f"pb{p}", xa[b], xs[b], gpost[0], "gpost0")

                def norm_wo(b):
                    norm_a(xa[b], xs[b], b)
                    norm_b(b, 16)
                pipeline3(4, emit_wo, post_wo, norm_wo)
                load_gpost(2048, True)

                def post_b(b, tb=tb, ti=ti, xa=xa, xs=xs):
                    r0 = tb * TT + b * 128
                    o = dma("sp", out[r0:r0 + 128, :], xa[b], [xs[b]], [f"out{tb * 4 + b}", f"osq{b}"], f"ost{b}")
                    store_ops.append(o)
                    refill(ti, b)

                def norm_next(b, ti=ti):
                    if ti + 1 < len(all_tiles):
                        first_norm(ti + 1, blocks=[b])
                ffn_tile(1, xa, xs, gpost[0], "gpost0", post_hook=post_b, norm_fn=norm_next)
                if tb + 1 < len(tilesB):
                    load_gpost(1024, False)

        S.finalize()
        esems = {n: es.enter_context(nc.semaphore(f"sem_{n}")) for n in ("pe", "act", "dve", "pool", "sp")}
        dsems = {k: es.enter_context(nc.semaphore(f"dsem_{k}")) for k in S.dma_keys}
        block = es.enter_context(nc.Block())

        @block.sync
        def _(e):
            S.emit("sp", e, esems, dsems, final_wait=store_ops)

        @block.gpsimd
        def _(e):
            S.emit("pool", e, esems, dsems)

        @block.scalar
        def _(e):
            S.emit("act", e, esems, dsems)

        @block.vector
        def _(e):
            S.emit("dve", e, esems, dsems)

        @block.tensor
        def _(e):
            S.emit("pe", e, esems, dsems)
    return nc


def _rope_tables():
    rows = SEQ // 64
    row = np.repeat(np.arange(rows, dtype=np.float32), 64)
    col = np.tile(np.arange(64, dtype=np.float32), rows)
    inv_freq = (np.float32(10000.0) ** (-np.arange(16, dtype=np.float32) / np.float32(16))).astype(np.float32)
    ang = np.stack([row[:, None] * inv_freq, col[:, None] * inv_freq], axis=1).astype(np.float32)
    cos = np.cos(ang).astype(np.float32)
    sin = np.sin(ang).astype(np.float32)
    cos_d = np.zeros((64, SEQ), np.float32)
    sin_d = np.zeros((64, SEQ), np.float32)
    for axis in range(2):
        for half in range(2):
            for f in range(16):
                d = axis * 32 + half * 16 + f
                cos_d[d] = cos[:, axis, f]
                sin_d[d] = sin[:, axis, f] * (-1.0 if half == 0 else 1.0)
    return np.concatenate([cos_d, cos_d], 0), np.concatenate([sin_d, sin_d], 0)


def _perm64():
    d = np.arange(64)
    return np.where((d % 32) < 16, d + 16, d - 16)


_PROGRAM_CACHE = {}


def kernel(**inputs):
    f = lambda k: np.asarray(inputs[k], dtype=np.float32)
    x = f("x")
    memv = f("mem")
    w_in = f("w_in")[0]
    perm = _perm64()
    qcols = np.concatenate([np.concatenate([np.arange(64) + (c + 4 * a) * 64 for a in range(2)]) for c in range(4)])
    qcols_p = np.concatenate([np.concatenate([perm + (c + 4 * a) * 64 for a in range(2)]) for c in range(4)])
    w_q = np.ascontiguousarray(np.concatenate([w_in[:, qcols], w_in[:, qcols_p]], axis=1))
    kc = 512 + np.arange(128)
    kc_p = 512 + np.concatenate([perm, perm + 64])
    w_kkv = np.ascontiguousarray(np.concatenate([w_in[:, kc], w_in[:, kc_p], w_in[:, 640:768],
                                                 w_in[:, 640:768]], axis=1))
    w_gu = np.ascontiguousarray(w_in[:, 768:1280])
    w_gv = np.ascontiguousarray(w_in[:, 1280:1792])
    w_qm = np.ascontiguousarray(w_in[:, 1792:2304])
    wbg = f("w_branch_gate")[0]
    bbg = f("b_branch_gate")[0]
    w_bg = np.ascontiguousarray(np.concatenate(
        [wbg[:, r * 1024 + c * 128: r * 1024 + (c + 1) * 128] for c in range(8) for r in range(3)], axis=1))
    wpa = f("w_proj_attn")[0][qcols]
    wps = [wpa, f("w_proj_gmlp")[0], f("w_proj_mem")[0]]
    w_pr = np.ascontiguousarray(np.concatenate(
        [wps[r][:, c * 128:(c + 1) * 128] for c in range(8) for r in range(3)], axis=1))
    cols = np.zeros((128, 64), np.float32)
    cols[:, 0:8] = f("ffn1_pre")[0].reshape(8, 128).T
    cols[:, 8:16] = f("mix_pre")[0].reshape(8, 128).T
    cols[:, 16:24] = f("ffn2_pre")[0].reshape(8, 128).T
    cols[:, 24:32] = f("mem_norm")[0].reshape(8, 128).T
    for c in range(8):
        for r in range(3):
            cols[:, 32 + c * 3 + r] = bbg[r * 1024 + c * 128: r * 1024 + (c + 1) * 128]
    gq, gk = f("q_norm")[0], f("k_norm")[0]
    cols[:, 56] = np.concatenate([gq, gq])
    cols[:, 57] = np.concatenate([gq[perm], gq[perm]])
    cols[:, 58] = np.concatenate([gk, gk])
    cols[:, 59] = np.concatenate([gk[perm], gk[perm]])
    rows = np.concatenate([f("ffn1_post")[0], f("mix_post")[0], f("ffn2_post")[0], f("gmlp_v_norm")[0],
                           f("gmlp_b_s")[0].reshape(-1)])[None, :].astype(np.float32)
    wsT = np.ascontiguousarray(np.transpose(f("gmlp_w_s")[0], (2, 0, 1)).reshape(128, 512))
    cos_t, sin_t = _rope_tables()
    shared = dict(
        wg1=f("ffn1_w_gate")[0], wu1=f("ffn1_w_up")[0], wd1=f("ffn1_w_down")[0],
        wg2=f("ffn2_w_gate")[0], wu2=f("ffn2_w_up")[0], wd2=f("ffn2_w_down")[0],
        w_kkv=w_kkv, w_q=w_q, w_gu=w_gu, w_gv=w_gv, w_qm=w_qm, w_bg=w_bg, w_pr=w_pr,
        w_out=f("w_out")[0], w_mkv=f("w_mem_kv")[0], cols=cols, rows=rows, wsT=wsT)
    shared = {k: np.ascontiguousarray(v) for k, v in shared.items()}
    in_maps = []
    for c in range(8):
        b, hf = c // 2, c % 2
        own = slice(hf * OWN, (hf + 1) * OWN)
        oth = slice((1 - hf) * OWN, (2 - hf) * OWN)
        xin = np.ascontiguousarray(np.concatenate([x[b, own], x[b, oth]], axis=0))
        rp = np.ascontiguousarray(np.stack([np.concatenate([cos_t[:, own], cos_t[:, oth]], axis=1),
                                            np.concatenate([sin_t[:, own], sin_t[:, oth]], axis=1)], axis=0))
        m = dict(shared)
        m.update(xin=xin, mem=np.ascontiguousarray(memv[b]), rope=rp)
        in_maps.append(m)
    if "full" not in _PROGRAM_CACHE:
        _PROGRAM_CACHE["full"] = build_program("full")
    nc = _PROGRAM_CACHE["full"]
    res = run_bass_kernel_spmd(nc, in_maps, core_ids=list(range(8)))
    outp = np.empty((4, SEQ, D), np.float32)
    for c in range(8):
        b, hf = c // 2, c % 2
        outp[b, hf * OWN:(hf + 1) * OWN] = res.results[c]["out"]
    return outp
```

```python
import numpy as np
import concourse.bass as bass
import concourse.mybir as mybir
from concourse.bass_utils import run_bass_kernel_spmd
from contextlib import ExitStack

F32 = mybir.dt.float32
BF16 = mybir.dt.bfloat16
AF = mybir.ActivationFunctionType
ALU = mybir.AluOpType

D = 1024
DFF = 2816
NJ = DFF // 128
SEQ = 4096
OWN = 2048
TT = 512
EPS = 1e-6
NRING = 6
NSTAT = 64


class Sched:
    def __init__(self):
        self.ops = []
        self.last_w = {}
        self.readers = {}

    def add(self, eng, fn, reads=(), writes=(), dma_key=None):
        idx = len(self.ops)
        deps = {}
        for r in reads:
            if r in self.last_w:
                deps[self.last_w[r]] = True
        for w in writes:
            if w in self.last_w:
                deps[self.last_w[w]] = True
            for rd in self.readers.get(w, ()):
                deps.setdefault(rd, False)
        deps.pop(idx, None)
        self.ops.append(dict(eng=eng, fn=fn, deps=deps, dma_key=dma_key, signal=False,
                             semval=None, dma_val=None))
        for w in writes:
            self.last_w[w] = idx
            self.readers[w] = []
        for r in reads:
            if r not in writes:
                self.readers.setdefault(r, []).append(idx)
        return idx

    def finalize(self):
        ops = self.ops
        for op in ops:
            need = []
            for d, hard in op["deps"].items():
                dop = ops[d]
                if dop["dma_key"] is None:
                    if dop["eng"] == op["eng"] and op["dma_key"] is None:
                        if op["eng"] == "pe":
                            continue
                    dop["signal"] = True
                need.append(d)
            op["need"] = need
        cnt = {}
        dcnt = {}
        for op in ops:
            if op["dma_key"] is not None:
                dcnt[op["dma_key"]] = dcnt.get(op["dma_key"], 0) + 16
                op["dma_val"] = dcnt[op["dma_key"]]
            elif op["signal"]:
                cnt[op["eng"]] = cnt.get(op["eng"], 0) + 1
                op["semval"] = cnt[op["eng"]]
        self.dma_keys = list(dcnt.keys())

    def emit(self, eng_name, eng, esems, dsems, final_wait=()):
        ops = self.ops
        known = {}

        def wait_for(d):
            dop = ops[d]
            if dop["dma_key"] is not None:
                sem, val, key = dsems[dop["dma_key"]], dop["dma_val"], ("d", dop["dma_key"])
            else:
                sem, val, key = esems[dop["eng"]], dop["semval"], ("e", dop["eng"])
            if known.get(key, 0) >= val:
                return
            eng.wait_ge(sem, val)
            known[key] = val

        for op in ops:
            if op["eng"] != eng_name:
                continue
            for d in sorted(op["need"]):
                wait_for(d)
            ins = op["fn"](eng)
            if op["dma_key"] is not None:
                ins.then_inc(dsems[op["dma_key"]], 16)
            elif op["signal"]:
                ins.then_inc(esems[eng_name], 1)
        for d in final_wait:
            wait_for(d)


def build_program(stage="full"):
    nc = bass.Bass("TRN2", target_bir_lowering=False)
    S = Sched()
    dram = {}

    def din(name, shape):
        dram[name] = nc.dram_tensor(name, list(shape), F32, kind="ExternalInput").ap()
        return dram[name]

    xin = din("xin", [SEQ, D])
    mem = din("mem", [256, D])
    rope = din("rope", [2, 128, SEQ])
    wg = [din("wg1", [D, DFF]), din("wg2", [D, DFF])]
    wu = [din("wu1", [D, DFF]), din("wu2", [D, DFF])]
    wdn = [din("wd1", [DFF, D]), din("wd2", [DFF, D])]
    w_kkv = din("w_kkv", [D, 512])
    w_q = din("w_q", [D, 1024])
    w_gu = din("w_gu", [D, 512])
    w_gv = din("w_gv", [D, 512])
    w_qm = din("w_qm", [D, 512])
    w_bg = din("w_bg", [D, 3072])
    w_pr = din("w_pr", [512, 3072])
    w_out = din("w_out", [D, D])
    w_mkv = din("w_mkv", [D, D])
    cols_d = din("cols", [128, 64])
    rows_d = din("rows", [1, 4096])
    wsT_d = din("wsT", [128, 512])
    out = nc.dram_tensor("out", [OWN, D], F32, kind="ExternalOutput").ap()
    x1s = nc.dram_tensor("x1s", [OWN, D], F32).ap()

    def kview(w):
        return w.rearrange("(k p) n -> p k n", p=128)

    with ExitStack() as es:
        def sb(name, shape, dt):
            return es.enter_context(nc.sbuf_tensor(name, list(shape), dt))

        def ps(name):
            return es.enter_context(nc.psum_tensor(name, [128, 2, 512], F32))

        xr = sb("xr", [128, NRING, D], F32)
        KT = sb("KT", [128, SEQ], BF16)
        VA = sb("VA", [128, 32, 128], BF16)
        VB = sb("VB", [128, 32, 128], BF16)
        wd = sb("wd", [128, NJ, D], BF16)
        NUNIT = 8
        wun = sb("wun", [128, NUNIT, 2048], BF16)
        hT = sb("hT", [128, 8, TT], BF16)
        AT = sb("AT", [128, NJ, TT], BF16)
        mx = sb("mx", [128, 3, 1024], BF16)
        vn = sb("vn", [128, 4, 512], BF16)
        htok = sb("htok", [128, 4, D], BF16)
        junk = sb("junk", [128, D], BF16)
        tbuf = [sb(f"tbuf{i}", [128, D], F32) for i in range(2)]
        sg = [sb(f"sg{i}", [128, 512], F32) for i in range(2)]
        cs = [sb("cs0", [128, 2, 512], F32)]
        gpost = [sb("gpost0", [128, D], F32)]
        gvn_b = sb("gvn_b", [128, 512], F32)
        bs_b = sb("bs_b", [128, 4, 128], F32)
        wsT = sb("wsT_sb", [128, 4, 128], BF16)
        KmT = sb("KmT", [128, 4, 256], BF16)
        Vm = sb("Vm", [128, 2, 512], BF16)
        cols = sb("cols_sb", [128, 64], F32)
        hb = sb("hb", [128, 24], F32)
        st = sb("st", [128, NSTAT], F32)
        sd = sb("sd", [128, NSTAT], F32)
        rs = sb("rs", [128, NSTAT], F32)
        ident = sb("ident", [128, 128], BF16)
        ones = sb("ones", [128, 128], BF16)
        bones = sb("bones", [128, 128], BF16)
        pa = [ps("pa0"), ps("pa1")]
        pb = [ps("pb0"), ps("pb1")]

        ws_slab = []
        WR = []
        mxf = [mx[:, i, :].bitcast(F32) for i in range(3)]
        QT = AT[:, 0:4, :]
        QmT = AT[:, 4:8, :]
        OT = AT[:, 8:12, :]
        OmT = AT[:, 12:16, :]
        gmT = AT[:, 16:20, :]
        sqb = AT[:, 20, :]

        state = dict(unit=0, stat=0, pa=0, pb=0, sgi=0, tb=0, csi=0, pi=0)

        def nxt(name, n):
            v = state[name]
            state[name] = (v + 1) % n
            return v

        def next_stat(n=1):
            v = state["stat"]
            if v + n > NSTAT:
                v = 0
            state["stat"] = v + n
            return v

        def dma(q, out_ap, in_ap, reads, writes, key):
            return S.add(q, lambda e: e.dma_start(out=out_ap, in_=in_ap), reads=reads, writes=writes,
                         dma_key=key)

        def alloc_units(n):
            u = state["unit"]
            if u + n > NUNIT:
                u = 0
            state["unit"] = (u + n) % NUNIT
            return u

        def load_slab(src_ap, view_fn=None):
            u = alloc_units(2)
            view = wun[:, u:u + 2, :].rearrange("p a n -> p (a n)").rearrange("p (k n) -> p k n", k=8)
            res = [f"wu{u}", f"wu{u + 1}"]
            dma("pool", view, src_ap, [], res, f"wu{u}")
            ws_slab.append(view)
            WR.append(res)
            return len(ws_slab) - 1

        def load_unit(src_ap, k):
            u = alloc_units(1)
            view = wun[:, u, :].rearrange("p (k n) -> p k n", k=k)
            res = [f"wu{u}"]
            dma("pool", view, src_ap, [], res, f"wu{u}")
            ws_slab.append(view)
            WR.append(res)
            return len(ws_slab) - 1

        def rstd_cols(c0, n, bias=EPS):
            S.add("act", lambda e: e.activation(out=sd[:, c0:c0 + n], in_=st[:, c0:c0 + n], func=AF.Ln,
                                                bias=bias, scale=1.0),
                  reads=[f"st{c}" for c in range(c0, c0 + n)], writes=[f"sd{c}" for c in range(c0, c0 + n)])
            S.add("act", lambda e: e.activation(out=rs[:, c0:c0 + n], in_=sd[:, c0:c0 + n], func=AF.Exp, scale=-0.5),
                  reads=[f"sd{c}" for c in range(c0, c0 + n)], writes=[f"rs{c}" for c in range(c0, c0 + n)])

        def norm_to_hT(xaps, xres, gcol0, nblk=4):
            c0 = next_stat(nblk)
            for b in range(nblk):
                S.add("act", lambda e, b=b: e.activation(out=junk[:, :], in_=xaps[b], func=AF.Square,
                                                         scale=1.0 / 32.0, accum_out=st[:, c0 + b:c0 + b + 1]),
                      reads=[xres[b]], writes=["junk", f"st{c0 + b}"])
            rstd_cols(c0, nblk)
            for b in range(nblk):
                S.add("dve", lambda e, b=b: e.tensor_scalar(out=htok[:, b, :], in0=xaps[b],
                                                            scalar1=rs[:, c0 + b:c0 + b + 1], scalar2=None,
                                                            op0=ALU.mult),
                      reads=[xres[b], f"rs{c0 + b}"], writes=[f"htok{b}"])
            for k in range(8):
                p = nxt("pb", 2)
                pt = pb[p][:, 0, :].bitcast(BF16)

                def tr(e, k=k, pt=pt):
                    for b in range(nblk):
                        i = e.transpose(out=pt[:, b * 128:(b + 1) * 128], in_=htok[:, b, k * 128:(k + 1) * 128],
                                        identity=ident[:, :])
                    return i
                S.add("pe", tr, reads=[f"htok{b}" for b in range(nblk)] + ["ident"], writes=[f"pb{p}"])
                S.add("dve", lambda e, k=k, pt=pt: e.tensor_scalar(
                    out=hT[:, k, 0:nblk * 128], in0=pt[:, 0:nblk * 128],
                    scalar1=cols[:, gcol0 + k:gcol0 + k + 1], scalar2=None, op0=ALU.mult),
                    reads=[f"pb{p}", "cols"], writes=[f"hT{k}"])

        HT_ALL = [f"hT{k}" for k in range(8)]
        HU_ALL = [f"hU{k}" for k in range(8)]
        _kmt_flat = KmT[:, :, :].rearrange("p a n -> p (a n)")

        def hu(k):
            if k < 4:
                return vn[:, k, :]
            if k < 6:
                return _kmt_flat[:, (k - 4) * 512:(k - 3) * 512]
            return Vm[:, k - 6, :]

        def norm_block(xap_b, xres_b, b, gcol0, dst_b=False, slot=None):
            norm_a(xap_b, xres_b, b, slot=slot)
            norm_b(b, gcol0, dst_b=dst_b, slot=slot)

        def norm_a(xap_b, xres_b, b, slot=None):
            sl = b if slot is None else slot
            c = next_stat(1)
            S.add("act", lambda e: e.activation(out=junk[:, :], in_=xap_b, func=AF.Square, scale=1.0 / 32.0,
                                                accum_out=st[:, c:c + 1]),
                  reads=[xres_b], writes=["junk", f"st{c}"])
            rstd_cols(c, 1)
            S.add("act", lambda e: e.activation(out=htok[:, sl, :], in_=xap_b, func=AF.Copy, scale=rs[:, c:c + 1]),
                  reads=[xres_b, f"rs{c}"], writes=[f"htok{sl}"])

        def norm_b(b, gcol0, deep=False, dst_b=False, slot=None):
            sl = b if slot is None else slot
            if deep:
                ptile, pname = [(pa[0], "pa0"), (pa[1], "pa1"), (pb[0], "pb0"), (pb[1], "pb1")][b % 4]
            else:
                p = nxt("pa", 2)
                ptile, pname = pa[p], f"pa{p}"
            pt = ptile[:, 0, :].bitcast(BF16)

            def tr(e):
                for k in range(8):
                    i = e.transpose(out=pt[:, k * 128:(k + 1) * 128], in_=htok[:, sl, k * 128:(k + 1) * 128],
                                    identity=ident[:, :])
                return i
            S.add("pe", tr, reads=[f"htok{sl}", "ident"], writes=[pname])
            if dst_b:
                ptk = pt.rearrange("p (k n) -> p k n", k=8)
                S.add("dve", lambda e: e.tensor_tensor(
                    out=vn[:, 0:4, b * 128:(b + 1) * 128], in0=ptk[:, 0:4, :],
                    in1=cols[:, gcol0:gcol0 + 4].unsqueeze(2).to_broadcast([128, 4, 128]), op=ALU.mult),
                    reads=[pname, "cols"], writes=HU_ALL[0:4])
                S.add("dve", lambda e: e.tensor_tensor(
                    out=_kmt_flat.rearrange("p (k n) -> p k n", k=2)[:, :, b * 128:(b + 1) * 128], in0=ptk[:, 4:6, :],
                    in1=cols[:, gcol0 + 4:gcol0 + 6].unsqueeze(2).to_broadcast([128, 2, 128]), op=ALU.mult),
                    reads=[pname, "cols"], writes=HU_ALL[4:6])
                S.add("dve", lambda e: e.tensor_tensor(
                    out=Vm[:, 0:2, b * 128:(b + 1) * 128], in0=ptk[:, 6:8, :],
                    in1=cols[:, gcol0 + 6:gcol0 + 8].unsqueeze(2).to_broadcast([128, 2, 128]), op=ALU.mult),
                    reads=[pname, "cols"], writes=HU_ALL[6:8])
                return
            S.add("dve", lambda e: e.tensor_tensor(
                out=hT[:, :, b * 128:(b + 1) * 128], in0=pt.rearrange("p (k n) -> p k n", k=8),
                in1=cols[:, gcol0:gcol0 + 8].unsqueeze(2).to_broadcast([128, 8, 128]), op=ALU.mult),
                reads=[pname, "cols"], writes=HT_ALL)

        def post_norm_residual(yps, yres, xap, xres, gp, gres):
            c = next_stat(1)
            yv = yps[:, :, :].rearrange("p a n -> p (a n)")
            S.add("act", lambda e: e.activation(out=junk[:, :], in_=yv, func=AF.Square, scale=1.0 / 32.0,
                                                accum_out=st[:, c:c + 1]),
                  reads=[yres], writes=["junk", f"st{c}"])
            rstd_cols(c, 1)
            t = nxt("tb", 2)
            S.add("dve", lambda e: e.scalar_tensor_tensor(out=tbuf[t][:, :], in0=yv, scalar=rs[:, c:c + 1],
                                                          in1=gp[:, :], op0=ALU.mult, op1=ALU.mult),
                  reads=[yres, f"rs{c}", gres], writes=[f"tbuf{t}"])
            S.add("dve", lambda e: e.tensor_tensor(out=xap, in0=xap, in1=tbuf[t][:, :], op=ALU.add),
                  reads=[xres, f"tbuf{t}"], writes=[xres])

        def pipeline3(n, pe_fn, post_fn, norm_fn, skew=2):
            for i in range(n + skew):
                if i < n:
                    pe_fn(i)
                if 0 <= i - 1 < n:
                    post_fn(i - 1)
                if 0 <= i - skew < n and norm_fn is not None:
                    norm_fn(i - skew)

        def ffn_tile(which, xaps, xres, gp, gres, post_hook=None, norm_fn=None, up_hook=None, pre_dn_hook=None):
            wgv, wuv = kview(wg[which]), kview(wu[which])
            for jj in range(NJ // 2):
                if jj == 2 and up_hook is not None:
                    up_hook()
                if wd_pending and wd_pending[0] == which:
                    for piece in (2 * jj, 2 * jj + 1):
                        if piece < NJ // 2:
                            load_wd_piece(which, piece)
                    if jj == NJ // 2 - 1:
                        wd_pending.clear()
                lg = load_unit(wgv[:, :, jj * 256:(jj + 1) * 256], 8)
                lu = load_unit(wuv[:, :, jj * 256:(jj + 1) * 256], 8)
                for c in range(2):
                    j = jj * 2 + c
                    p = nxt("pa", 2)

                    def mm(e, lg=lg, lu=lu, c=c, p=p):
                        for a, l in ((0, lg), (1, lu)):
                            for k in range(8):
                                i = e.matmul(pa[p][:, a, :], lhsT=ws_slab[l][:, k, c * 128:(c + 1) * 128],
                                             rhs=hT[:, k, :], start=(k == 0), stop=(k == 7))
                        return i
                    S.add("pe", mm, reads=WR[lg] + WR[lu] + HT_ALL, writes=[f"pa{p}"])
                    g = nxt("sgi", 2)
                    S.add("act", lambda e, p=p, g=g: e.activation(out=sg[g][:, :], in_=pa[p][:, 0, :], func=AF.Silu),
                          reads=[f"pa{p}"], writes=[f"sg{g}"])
                    S.add("dve", lambda e, p=p, g=g, j=j: e.tensor_tensor(out=AT[:, j, :], in0=sg[g][:, :],
                                                                         in1=pa[p][:, 1, :], op=ALU.mult),
                          reads=[f"sg{g}", f"pa{p}"], writes=[f"at{j}"])
            pbs = {}

            def emit_dn(b):
                p = nxt("pb", 2)
                pbs[b] = p

                def dn(e, b=b, p=p):
                    for a in range(2):
                        for j in range(NJ):
                            i = e.matmul(pb[p][:, a, :], lhsT=AT[:, j, b * 128:(b + 1) * 128],
                                         rhs=wd[:, j, a * 512:(a + 1) * 512], start=(j == 0), stop=(j == NJ - 1))
                    return i
                S.add("pe", dn, reads=[f"at{j}" for j in range(NJ)] + WD_ALL, writes=[f"pb{p}"])

            def finish(b):
                p = pbs[b]
                post_norm_residual(pb[p], f"pb{p}", xaps[b], xres[b], gp, gres)
                if post_hook is not None:
                    post_hook(b)
            if pre_dn_hook is not None:
                pre_dn_hook()
            pipeline3(4, emit_dn, finish, norm_fn, skew=1)

        WD_ALL = [f"wd{i}" for i in range(NJ // 2)]

        def load_wd_piece(which, i):
            v = wdn[which].rearrange("(j p) n -> p j n", p=128)
            dma("pool", wd[:, 2 * i:2 * i + 2, :], v[:, 2 * i:2 * i + 2, :], [], [f"wd{i}"], f"wd{i}")

        def load_row_table(dst, dres, off, n, key):
            dma("sp", dst, rows_d[:, off:off + n].partition_broadcast(128), [], [dres], key)

        def rope_norm(pq, pres, gc, gpc, csb, csres, out_ap, out_res, split=None):
            S.add("act", lambda e: e.activation(out=sqb, in_=pq[:, 0, :], func=AF.Square, scale=0.125),
                  reads=[pres], writes=["at20"])
            p2 = nxt("pb", 2)
            S.add("pe", lambda e: e.matmul(pb[p2][:, 0, :], lhsT=bones[:, :], rhs=sqb, start=True, stop=True),
                  reads=["at20", "bones"], writes=[f"pb{p2}"])
            S.add("act", lambda e: e.activation(out=mxf[2], in_=pb[p2][:, 0, :], func=AF.Ln, bias=EPS, scale=1.0),
                  reads=[f"pb{p2}"], writes=["mx2"])
            S.add("act", lambda e: e.activation(out=mxf[2], in_=mxf[2], func=AF.Exp, scale=-0.5),
                  reads=["mx2"], writes=["mx2"])
            S.add("dve", lambda e: e.scalar_tensor_tensor(out=mxf[0], in0=pq[:, 0, :], scalar=cols[:, gc:gc + 1],
                                                          in1=csb[:, 0, :], op0=ALU.mult, op1=ALU.mult),
                  reads=[pres, csres, "cols", "at20"], writes=["mx0"])
            S.add("dve", lambda e: e.scalar_tensor_tensor(out=mxf[1], in0=pq[:, 1, :], scalar=cols[:, gpc:gpc + 1],
                                                          in1=csb[:, 1, :], op0=ALU.mult, op1=ALU.mult),
                  reads=[pres, csres, "cols"], writes=["mx1"])
            S.add("dve", lambda e: e.tensor_tensor(out=mxf[0], in0=mxf[0], in1=mxf[1], op=ALU.add),
                  reads=["mx0", "mx1"], writes=["mx0"])
            if split is None:
                S.add("dve", lambda e: e.tensor_tensor(out=out_ap, in0=mxf[0], in1=mxf[2], op=ALU.mult),
                      reads=["mx0", "mx2"], writes=[out_res])
            else:
                (oa, ra), (ob, rb) = split
                S.add("dve", lambda e: e.tensor_tensor(out=oa, in0=mxf[0][0:64, :], in1=mxf[2][0:64, :], op=ALU.mult),
                      reads=["mx0", "mx2"], writes=[ra])
                S.add("dve", lambda e: e.tensor_tensor(out=ob, in0=mxf[0][64:128, :], in1=mxf[2][64:128, :],
                                                       op=ALU.mult),
                      reads=["mx0", "mx2"], writes=[rb])

        def load_rope_tile(tok0):
            c = nxt("csi", 1)
            dma("sp", cs[c][:, :, :], rope[:, :, tok0:tok0 + TT].rearrange("t p n -> p t n"), [], [f"cs{c}"],
                f"cs{c}")
            return c

        S.add("pool", lambda e: e.memset(ident[:, :], 1.0), writes=["ident"])
        S.add("pool", lambda e: e.affine_select(out=ident[:, :], in_=ident[:, :], pattern=[[-1, 128]],
                                                compare_op=ALU.is_equal, fill=0.0, base=0, channel_multiplier=1),
              reads=["ident"], writes=["ident"])
        S.add("pool", lambda e: e.memset(ones[:, :], 1.0), writes=["ones"])
        S.add("pool", lambda e: e.memset(bones[:, :], 0.0), writes=["bones"])
        S.add("pool", lambda e: e.memset(bones[0:64, 0:64], 1.0), reads=["bones"], writes=["bones"])
        S.add("pool", lambda e: e.memset(bones[64:128, 64:128], 1.0), reads=["bones"], writes=["bones"])
        S.add("dve", lambda e: e.memset(st[:, :], 0.0), writes=[f"st{c}" for c in range(NSTAT)])
        S.add("dve", lambda e: e.memset(VA[:, :, 64:128], 1.0), writes=[f"VA{t}" for t in range(SEQ // TT)])
        S.add("dve", lambda e: e.memset(VB[:, :, 0:64], 1.0), writes=[f"VB{t}" for t in range(SEQ // TT)])
        dma("sp", cols[:, :], cols_d[:, :], [], ["cols"], "cols")
        load_row_table(gvn_b[:, :], "gvn_b", 3072, 512, "gvn_b")
        load_row_table(bs_b[:, :, :].rearrange("p g n -> p (g n)"), "bs_b", 3584, 512, "bs_b")
        dma("pool", wsT[:, :, :].rearrange("p g n -> p (g n)"), wsT_d[:, :], [], ["wsT"], "wsT")
        S.add("dve", lambda e: e.tensor_scalar(out=hb[:, :], in0=cols[:, 32:56], scalar1=0.5, scalar2=None,
                                               op0=ALU.mult), reads=["cols"], writes=["hb"])

        def ring_slot(gblk):
            return gblk % NRING

        def xap(gblk):
            return xr[:, ring_slot(gblk), :]

        def xres(gblk):
            return f"xr{ring_slot(gblk)}"

        xsrc = {}

        def load_block(gblk):
            src, rd = xsrc[gblk]
            s = ring_slot(gblk)
            dma("sp", xr[:, s, :], src, rd, [f"xr{s}"], f"xr{s}")

        nA = SEQ // TT if stage == "full" else (1 if stage == "A1" else OWN // TT)
        nB = OWN // TT if stage == "full" else 0
        gb = 0
        tilesA, tilesB = [], []
        for t in range(nA):
            blks = []
            for b in range(4):
                r0 = t * TT + b * 128
                xsrc[gb] = (xin[r0:r0 + 128, :], [])
                blks.append(gb)
                gb += 1
            tilesA.append(blks)
        for t in range(nB):
            blks = []
            for b in range(4):
                r0 = t * TT + b * 128
                xsrc[gb] = (x1s[r0:r0 + 128, :], [f"x1s{t * 4 + b}"])
                blks.append(gb)
                gb += 1
            tilesB.append(blks)
        all_tiles = tilesA + tilesB
        loaded = set()

        def refill(ti, b):
            tgt = (ti + 1, b + 2) if b < 2 else (ti + 2, b - 2)
            if tgt[0] < len(all_tiles):
                g = all_tiles[tgt[0]][tgt[1]]
                if g not in loaded:
                    load_block(g)
                    loaded.add(g)

        def ensure_loaded(ti):
            for g in all_tiles[ti]:
                if g not in loaded:
                    load_block(g)
                    loaded.add(g)
            if ti + 1 < len(all_tiles):
                for g in all_tiles[ti + 1][:2]:
                    if g not in loaded:
                        load_block(g)
                        loaded.add(g)

        def mem_kv():
            for b in range(2):
                dma("sp", tbuf[b][:, :], mem[b * 128:(b + 1) * 128, :], [], [f"tbuf{b}"], f"tbuf{b}")
            for b in range(2):
                norm_a(tbuf[b][:, :], f"tbuf{b}", b)
            for b in range(2):
                norm_b(b, 24)
            s0 = load_slab(kview(w_mkv)[:, :, 0:512])
            s1 = load_slab(kview(w_mkv)[:, :, 512:1024])
            for m in range(4):
                p = nxt("pa", 2)

                def mmk(e, m=m, p=p):
                    for k in range(8):
                        i = e.matmul(pa[p][:, 0, 0:256], lhsT=ws_slab[s0][:, k, m * 128:(m + 1) * 128],
                                     rhs=hT[:, k, 0:256], start=(k == 0), stop=(k == 7))
                    return i
                S.add("pe", mmk, reads=[*WR[s0]] + HT_ALL, writes=[f"pa{p}"])
                S.add("dve", lambda e, m=m, p=p: e.tensor_copy(out=KmT[:, m, :], in_=pa[p][:, 0, 0:256]),
                      reads=[f"pa{p}"], writes=["KmT"])
            for b in range(2):
                p = nxt("pb", 2)

                def mmv(e, b=b, p=p):
                    for k in range(8):
                        i = e.matmul(pb[p][:, 0, :], lhsT=hT[:, k, b * 128:(b + 1) * 128], rhs=ws_slab[s1][:, k, :],
                                     start=(k == 0), stop=(k == 7))
                    return i
                S.add("pe", mmv, reads=[*WR[s1]] + HT_ALL, writes=[f"pb{p}"])
                S.add("dve", lambda e, b=b, p=p: e.tensor_copy(out=Vm[:, b, :], in_=pb[p][:, 0, :]),
                      reads=[f"pb{p}"], writes=["Vm"])


        tile_gcol = [0] * len(tilesA) + [8] * len(tilesB)
        done_a, done_b = set(), set()

        def first_norm(ti, only_a=False, blocks=range(4)):
            for b in blocks:
                if (ti, b) not in done_a:
                    g = all_tiles[ti][b]
                    if g not in loaded:
                        load_block(g)
                        loaded.add(g)
                    norm_a(xap(g), xres(g), b)
                    done_a.add((ti, b))
            if only_a:
                return
            for b in blocks:
                if (ti, b) not in done_b:
                    norm_b(b, tile_gcol[ti], deep=(len(list(blocks)) == 4))
                    done_b.add((ti, b))

        wd_pending = [0]
        def load_gpost(off, half):
            load_row_table(gpost[0][:, :], "gpost0", off, 1024, "gpost0")
            if half:
                S.add("dve", lambda e: e.tensor_scalar(out=gpost[0][:, :], in0=gpost[0][:, :], scalar1=0.5,
                                                       scalar2=None, op0=ALU.mult),
                      reads=["gpost0"], writes=["gpost0"])
        load_gpost(0, True)
        pending_kv = []

        def kv_stage_body(ti):
            c = load_rope_tile(ti * TT)
            s = load_slab(kview(w_kkv))
            p = nxt("pa", 2)

            def mmk2(e, s=s, p=p):
                for a in range(2):
                    for k in range(8):
                        i = e.matmul(pa[p][:, a, :], lhsT=ws_slab[s][:, k, a * 128:(a + 1) * 128], rhs=hu(k),
                                     start=(k == 0), stop=(k == 7))
                return i
            S.add("pe", mmk2, reads=[*WR[s]] + HU_ALL, writes=[f"pa{p}"])
            pk = p
            p = nxt("pb", 2)

            def mmv2(e, s=s, p=p):
                for b in range(4):
                    for k in range(8):
                        i = e.matmul(pb[p][:, 0, b * 128:(b + 1) * 128], lhsT=hu(k)[:, b * 128:(b + 1) * 128],
                                     rhs=ws_slab[s][:, k, 256:384], start=(k == 0), stop=(k == 7))
                return i
            S.add("pe", mmv2, reads=[*WR[s]] + HU_ALL, writes=[f"pb{p}"])
            S.add("dve", lambda e, p=p, ti=ti: e.tensor_copy(
                out=VA[:, ti * 4:(ti + 1) * 4, 0:64],
                in_=pb[p][:, 0, :].rearrange("p (b n) -> p b n", b=4)[:, :, 0:64]),
                reads=[f"pb{p}"], writes=[f"VA{ti}"])
            S.add("dve", lambda e, p=p, ti=ti: e.tensor_copy(
                out=VB[:, ti * 4:(ti + 1) * 4, 64:128],
                in_=pb[p][:, 0, :].rearrange("p (b n) -> p b n", b=4)[:, :, 64:128]),
                reads=[f"pb{p}"], writes=[f"VB{ti}"])
            rope_norm(pa[pk], f"pa{pk}", 58, 59, cs[c], f"cs{c}", KT[:, ti * TT:(ti + 1) * TT], f"KT{ti}")

        store_ops = []
        for ti, blks in enumerate(tilesA):
            own = ti < OWN // TT
            ensure_loaded(ti)
            xa = [xap(g) for g in blks]
            xs = [xres(g) for g in blks]
            first_norm(ti)

            def post_a(b, ti=ti, own=own, xa=xa, xs=xs):
                if own:
                    r0 = ti * TT + b * 128
                    dst = x1s if stage == "full" else out
                    o = dma("sp", dst[r0:r0 + 128, :], xa[b], [xs[b]], [f"x1s{ti * 4 + b}", f"stq{b}"], f"x1st{b}")
                    store_ops.append(o)
                if stage != "full":
                    refill(ti, b)

            has_next = stage == "full" and ti + 1 < len(tilesA)

            def next_n1(b, slot, ti=ti):
                g = all_tiles[ti + 1][b]
                if g not in loaded:
                    load_block(g)
                    loaded.add(g)
                norm_a(xap(g), xres(g), b, slot=slot)
                norm_b(b, 0, slot=slot)
                done_a.add((ti + 1, b))
                done_b.add((ti + 1, b))

            def pre_dn(has_next=has_next):
                if has_next:
                    next_n1(0, 0)
                    next_n1(1, 1)

            def norm_a2(b, ti=ti, xa=xa, xs=xs, has_next=has_next):
                if b < 3 or not has_next:
                    norm_block(xa[b], xs[b], b, 8, dst_b=True)
                else:
                    norm_a(xa[b], xs[b], b)
                refill(ti, b)
                if b == 2 and has_next:
                    next_n1(2, 0)
                    next_n1(3, 1)
            ffn_tile(0, xa, xs, gpost[0], "gpost0", post_hook=post_a, norm_fn=norm_a2 if stage == "full" else None,
                     up_hook=pending_kv.pop() if pending_kv else None, pre_dn_hook=pre_dn)
            if stage != "full":
                continue

            def kv_stage(ti=ti, deferred=has_next):
                if deferred:
                    norm_b(3, 8, dst_b=True)
                kv_stage_body(ti)
            if has_next:
                pending_kv.append(kv_stage)
            else:
                kv_stage()

        if stage == "full":
            S.add("dve", lambda e: e.memset(sd[:, 0:1], 0.0), reads=HU_ALL,
                  writes=["vn0", "vn1", "vn2", "vn3", "KmT", "Vm", "sd0"])
            mem_kv()
            wd_pending.append(1)
            load_gpost(1024, False)
            KT_ALL = [f"KT{t}" for t in range(SEQ // TT)]
            V_ALL = [f"VA{t}" for t in range(SEQ // TT)] + [f"VB{t}" for t in range(SEQ // TT)]
            SC_A = 0.125
            SC_M = 128.0 ** -0.5
            for tb, blks in enumerate(tilesB):
                ti = len(tilesA) + tb
                ensure_loaded(ti)
                xa = [xap(g) for g in blks]
                xs = [xres(g) for g in blks]
                first_norm(ti)
                c = load_rope_tile(tb * TT)
                sv = load_slab(kview(w_gv))
                c0 = next_stat(4)
                for b in range(4):
                    p = nxt("pb", 2)

                    def mmgv(e, b=b, p=p, sv=sv):
                        for k in range(8):
                            i = e.matmul(pb[p][:, 0, :], lhsT=hT[:, k, b * 128:(b + 1) * 128], rhs=ws_slab[sv][:, k, :],
                                         start=(k == 0), stop=(k == 7))
                        return i
                    S.add("pe", mmgv, reads=[*WR[sv]] + HT_ALL, writes=[f"pb{p}"])
                    gbuf = tbuf[b // 2][:, (b % 2) * 512:(b % 2 + 1) * 512]
                    S.add("act", lambda e, p=p, gbuf=gbuf: e.activation(out=gbuf, in_=pb[p][:, 0, :],
                                                                        func=AF.Gelu_apprx_tanh),
                          reads=[f"pb{p}"], writes=[f"tbuf{b // 2}"])
                    S.add("act", lambda e, gbuf=gbuf, b=b, c0=c0: e.activation(
                        out=junk[:, 0:512], in_=gbuf, func=AF.Square, scale=512.0 ** -0.5,
                        accum_out=st[:, c0 + b:c0 + b + 1]),
                        reads=[f"tbuf{b // 2}"], writes=["junk", f"st{c0 + b}"])
                rstd_cols(c0, 4)
                for b in range(4):
                    gbuf = tbuf[b // 2][:, (b % 2) * 512:(b % 2 + 1) * 512]
                    S.add("dve", lambda e, gbuf=gbuf, b=b, c0=c0: e.scalar_tensor_tensor(
                        out=vn[:, b, :], in0=gbuf, scalar=rs[:, c0 + b:c0 + b + 1], in1=gvn_b[:, :],
                        op0=ALU.mult, op1=ALU.mult),
                        reads=[f"tbuf{b // 2}", f"rs{c0 + b}", "gvn_b"], writes=[f"vn{b}"])
                sm = load_slab(kview(w_qm))
                for m in range(4):
                    p = nxt("pa", 2)

                    def mmqm(e, m=m, p=p, sm=sm):
                        for k in range(8):
                            i = e.matmul(pa[p][:, 0, :], lhsT=ws_slab[sm][:, k, m * 128:(m + 1) * 128],
                                         rhs=hT[:, k, :], start=(k == 0), stop=(k == 7))
                        return i
                    S.add("pe", mmqm, reads=[*WR[sm]] + HT_ALL, writes=[f"pa{p}"])
                    S.add("dve", lambda e, m=m, p=p: e.tensor_copy(out=QmT[:, m, :], in_=pa[p][:, 0, :]),
                          reads=[f"pa{p}"], writes=[f"at{4 + m}"])
                su = load_slab(kview(w_gu))
                for ch in range(4):
                    p = nxt("pa", 2)

                    def mmgu(e, ch=ch, p=p, su=su):
                        for k in range(8):
                            i = e.matmul(pa[p][:, 0, :], lhsT=ws_slab[su][:, k, ch * 128:(ch + 1) * 128],
                                         rhs=hT[:, k, :], start=(k == 0), stop=(k == 7))
                        for b in range(4):
                            i = e.matmul(pa[p][:, 1, b * 128:(b + 1) * 128], lhsT=vn[:, b, ch * 128:(ch + 1) * 128],
                                         rhs=wsT[:, ch, :], start=True, stop=True)
                        return i
                    S.add("pe", mmgu, reads=[*WR[su], "wsT"] + HT_ALL + [f"vn{b}" for b in range(4)],
                          writes=[f"pa{p}"])
                    g = nxt("sgi", 2)
                    S.add("act", lambda e, p=p, g=g: e.activation(out=sg[g][:, :], in_=pa[p][:, 0, :],
                                                                  func=AF.Gelu_apprx_tanh),
                          reads=[f"pa{p}"], writes=[f"sg{g}"])
                    t = nxt("tb", 2)
                    for b in range(4):
                        S.add("dve", lambda e, p=p, t=t, b=b, ch=ch: e.tensor_tensor(
                            out=tbuf[t][:, b * 128:(b + 1) * 128], in0=pa[p][:, 1, b * 128:(b + 1) * 128],
                            in1=bs_b[:, ch, :], op=ALU.add),
                            reads=[f"pa{p}", "bs_b"], writes=[f"tbuf{t}"])
                    S.add("dve", lambda e, g=g, t=t, ch=ch: e.scalar_tensor_tensor(
                        out=gmT[:, ch, :], in0=sg[g][:, :], scalar=0.5, in1=tbuf[t][:, 0:512], op0=ALU.mult,
                        op1=ALU.mult), reads=[f"sg{g}", f"tbuf{t}"], writes=[f"at{16 + ch}"])
                S.add("dve", lambda e: e.memset(AT[64:128, 0:4, :], 0.0), writes=[f"at{k}" for k in range(4)])
                S.add("dve", lambda e: e.memset(vn[0:64, :, :], 0.0), writes=[f"vn{k}" for k in range(4)])
                sq0 = load_slab(kview(w_q)[:, :, 0:512])
                sq1 = load_slab(kview(w_q)[:, :, 512:1024])
                qps = []
                for ch in range(4):
                    p = nxt("pa", 2)

                    def mmq(e, ch=ch, p=p, sq0=sq0, sq1=sq1):
                        for a, sl in ((0, sq0), (1, sq1)):
                            for k in range(8):
                                i = e.matmul(pa[p][:, a, :], lhsT=ws_slab[sl][:, k, ch * 128:(ch + 1) * 128],
                                             rhs=hT[:, k, :], start=(k == 0), stop=(k == 7))
                        return i
                    S.add("pe", mmq, reads=[*WR[sq0], *WR[sq1]] + HT_ALL, writes=[f"pa{p}"])
                    qps.append(p)
                    if ch >= 1:
                        pp = qps[ch - 1]
                        rope_norm(pa[pp], f"pa{pp}", 56, 57, cs[c], f"cs{c}", None, None,
                                  split=((AT[0:64, ch - 1, :], f"at{ch - 1}"), (vn[64:128, ch - 1, :], f"vn{ch - 1}")))
                pp = qps[3]
                rope_norm(pa[pp], f"pa{pp}", 56, 57, cs[c], f"cs{c}", None, None,
                          split=((AT[0:64, 3, :], "at3"), (vn[64:128, 3, :], "vn3")))
                for hp in range(4):
                    po = nxt("pb", 2)

                    sring = [(pa[0], "pa0"), (pa[1], "pa1"), (pb[1 - po], f"pb{1 - po}")]

                    def qk(kb, hp=hp, sring=sring):
                        pst, psn = sring[kb % 3]

                        def f(e, kb=kb, pst=pst):
                            e.matmul(pst[:, 0, :], lhsT=KT[:, kb * 128:(kb + 1) * 128], rhs=AT[:, hp, :],
                                     start=True, stop=True)
                            return e.matmul(pst[:, 1, :], lhsT=KT[:, kb * 128:(kb + 1) * 128],
                                            rhs=vn[:, hp, :], start=True, stop=True)
                        S.add("pe", f, reads=KT_ALL + [f"at{hp}", f"vn{hp}"], writes=[psn])
                        pi = nxt("pi", 3)
                        S.add("act", lambda e, pst=pst, pi=pi: e.activation(
                            out=mx[:, pi, :], in_=pst[:, :, :].rearrange("p a n -> p (a n)"), func=AF.Exp,
                            scale=SC_A), reads=[psn], writes=[f"mx{pi}"])
                        return pi

                    def pv(kb, pi, po=po):
                        def f(e, kb=kb, pi=pi):
                            st_, sp_ = (kb == 0), (kb == 31)
                            e.matmul(pb[po][:, 0, :], lhsT=VA[:, kb, :], rhs=mx[:, pi, 0:512], start=st_, stop=sp_)
                            return e.matmul(pb[po][:, 1, :], lhsT=VB[:, kb, :], rhs=mx[:, pi, 512:1024], start=st_,
                                            stop=sp_)
                        S.add("pe", f, reads=V_ALL + [f"mx{pi}"], writes=[f"pb{po}"])

                    pis = {}
                    for kb in range(32 + 2):
                        if kb < 32:
                            pis[kb] = qk(kb)
                        if kb - 2 >= 0:
                            pv(kb - 2, pis[kb - 2])
                    g = nxt("sgi", 2)
                    t = nxt("tb", 2)
                    S.add("dve", lambda e, po=po, t=t: e.tensor_copy(
                        out=tbuf[t][:, :], in_=pb[po][:, :, :].rearrange("p a n -> p (a n)")),
                        reads=[f"pb{po}"], writes=[f"tbuf{t}"])
                    dma("sp", sg[g][0:64, :], tbuf[t][64:128, 0:512], [f"tbuf{t}"], [f"sg{g}"], f"sgd{g}")
                    dma("sp", sg[g][64:128, :], tbuf[t][0:64, 512:1024], [f"tbuf{t}"], [f"sg{g}", f"sg{g}x"],
                        f"sgd{g}x")
                    if hp == 3:
                        S.add("act", lambda e, g=g: e.activation(out=sg[g][:, :], in_=sg[g][:, :], func=AF.Ln),
                              reads=[f"sg{g}", f"sg{g}x"], writes=[f"sg{g}"])
                        S.add("act", lambda e, g=g: e.activation(out=sg[g][:, :], in_=sg[g][:, :], func=AF.Exp,
                                                                 scale=-1.0),
                              reads=[f"sg{g}"], writes=[f"sg{g}"])
                    else:
                        S.add("dve", lambda e, g=g: e.reciprocal(out=sg[g][:, :], in_=sg[g][:, :]),
                              reads=[f"sg{g}", f"sg{g}x"], writes=[f"sg{g}"])
                    S.add("dve", lambda e, t=t, g=g, hp=hp: e.scalar_tensor_tensor(
                        out=OT[0:64, hp, :], in0=tbuf[t][0:64, 0:512], scalar=0.5, in1=sg[g][0:64, :], op0=ALU.mult,
                        op1=ALU.mult), reads=[f"tbuf{t}", f"sg{g}"], writes=[f"at{8 + hp}"])
                    S.add("dve", lambda e, t=t, g=g, hp=hp: e.scalar_tensor_tensor(
                        out=OT[64:128, hp, :], in0=tbuf[t][64:128, 512:1024], scalar=0.5, in1=sg[g][64:128, :],
                        op0=ALU.mult, op1=ALU.mult), reads=[f"tbuf{t}", f"sg{g}"], writes=[f"at{8 + hp}"])
                mem_state = {}

                def mem_s(m):
                    p = nxt("pa", 2)

                    def mms(e, m=m, p=p):
                        e.matmul(pa[p][:, 0, :], lhsT=KmT[:, m, 0:128], rhs=QmT[:, m, :], start=True, stop=True)
                        return e.matmul(pa[p][:, 1, :], lhsT=KmT[:, m, 128:256], rhs=QmT[:, m, :], start=True, stop=True)
                    S.add("pe", mms, reads=["KmT", f"at{4 + m}"], writes=[f"pa{p}"])
                    pi = nxt("pi", 3)
                    S.add("act", lambda e, p=p, pi=pi: e.activation(
                        out=mx[:, pi, :], in_=pa[p][:, :, :].rearrange("p a n -> p (a n)"), func=AF.Exp, scale=SC_M),
                        reads=[f"pa{p}"], writes=[f"mx{pi}"])
                    mem_state[m] = pi

                def mem_o(m):
                    pi = mem_state[m]
                    po = nxt("pb", 2)

                    def mmo(e, m=m, pi=pi, po=po):
                        e.matmul(pb[po][:, 0, :], lhsT=Vm[:, 0, m * 128:(m + 1) * 128], rhs=mx[:, pi, 0:512],
                                 start=True, stop=False)
                        e.matmul(pb[po][:, 0, :], lhsT=Vm[:, 1, m * 128:(m + 1) * 128], rhs=mx[:, pi, 512:1024],
                                 start=False, stop=True)
                        e.matmul(pb[po][:, 1, :], lhsT=ones[:, :], rhs=mx[:, pi, 0:512], start=True, stop=False)
                        return e.matmul(pb[po][:, 1, :], lhsT=ones[:, :], rhs=mx[:, pi, 512:1024], start=False,
                                        stop=True)
                    S.add("pe", mmo, reads=["Vm", f"mx{pi}", "ones"], writes=[f"pb{po}"])
                    g = nxt("sgi", 2)
                    S.add("act", lambda e, po=po, g=g: e.activation(out=sg[g][:, :], in_=pb[po][:, 1, :], func=AF.Ln),
                          reads=[f"pb{po}"], writes=[f"sg{g}"])
                    S.add("act", lambda e, g=g: e.activation(out=sg[g][:, :], in_=sg[g][:, :], func=AF.Exp, scale=-1.0),
                          reads=[f"sg{g}"], writes=[f"sg{g}"])
                    S.add("dve", lambda e, po=po, g=g, m=m: e.scalar_tensor_tensor(
                        out=OmT[:, m, :], in0=pb[po][:, 0, :], scalar=0.5, in1=sg[g][:, :], op0=ALU.mult,
                        op1=ALU.mult), reads=[f"pb{po}", f"sg{g}"], writes=[f"at{12 + m}"])
                mem_s(0)
                for m in range(1, 4):
                    mem_s(m)
                    mem_o(m - 1)
                mem_o(3)
                brT = [OT, gmT, OmT]
                brres = [[f"at{8 + k}" for k in range(4)], [f"at{16 + k}" for k in range(4)],
                         [f"at{12 + k}" for k in range(4)]]
                gslot = None
                pslot = None
                macc = mxf[0]
                mtmp = mxf[1]
                for cc in range(8):
                    for r in range(3):
                        q = cc * 3 + r
                        if q % 4 == 0:
                            gslot = load_slab(kview(w_bg)[:, :, (q // 4) * 512:(q // 4 + 1) * 512])
                            pslot = load_unit(w_pr.rearrange("(k p) n -> p k n", p=128)[:, :, (q // 4) * 512:
                                                                                        (q // 4 + 1) * 512], 4)
                        pt4, pres = [(pa[0], "pa0"), (pa[1], "pa1"), (pb[0], "pb0"), (pb[1], "pb1")][q % 4]

                        def mmg(e, q=q, r=r, pt4=pt4, gslot=gslot, pslot=pslot):
                            for k in range(8):
                                e.matmul(pt4[:, 0, :], lhsT=ws_slab[gslot][:, k, (q % 4) * 128:(q % 4 + 1) * 128],
                                         rhs=hT[:, k, :], start=(k == 0), stop=(k == 7))
                            for k in range(4):
                                i = e.matmul(pt4[:, 1, :], lhsT=ws_slab[pslot][:, k, (q % 4) * 128:(q % 4 + 1) * 128],
                                             rhs=brT[r][:, k, :], start=(k == 0), stop=(k == 3))
                            return i
                        S.add("pe", mmg, reads=[*WR[gslot], *WR[pslot]] + HT_ALL + brres[r], writes=[pres])
                        g = nxt("sgi", 2)
                        S.add("act", lambda e, pt4=pt4, g=g, q=q: e.activation(out=sg[g][:, :], in_=pt4[:, 0, :],
                                                                               func=AF.Tanh, bias=hb[:, q:q + 1],
                                                                               scale=0.5),
                              reads=[pres, "hb"], writes=[f"sg{g}"])
                        dst = macc if r == 0 else mtmp
                        dres = "mx0" if r == 0 else "mx1"
                        S.add("dve", lambda e, pt4=pt4, g=g, dst=dst: e.scalar_tensor_tensor(
                            out=dst, in0=sg[g][:, :], scalar=1.0, in1=pt4[:, 1, :], op0=ALU.add, op1=ALU.mult),
                            reads=[f"sg{g}", pres], writes=[dres])
                        if r == 1:
                            S.add("dve", lambda e: e.tensor_tensor(out=macc, in0=macc, in1=mtmp, op=ALU.add),
                                  reads=["mx0", "mx1"], writes=["mx0"])
                        if r == 2:
                            S.add("dve", lambda e, cc=cc: e.tensor_tensor(out=AT[:, cc, :], in0=macc, in1=mtmp,
                                                                         op=ALU.add),
                                  reads=["mx0", "mx1"], writes=[f"at{cc}"])
                so0 = load_slab(kview(w_out)[:, :, 0:512])
                so1 = load_slab(kview(w_out)[:, :, 512:1024])
                wps = {}

                def emit_wo(b):
                    p = nxt("pb", 2)
                    wps[b] = p

                    def mmo2(e, b=b, p=p, so0=so0, so1=so1):
                        for a, sl in ((0, so0), (1, so1)):
                            for k in range(8):
                                i = e.matmul(pb[p][:, a, :], lhsT=AT[:, k, b * 128:(b + 1) * 128], rhs=ws_slab[sl][:, k, :],
                                             start=(k == 0), stop=(k == 7))
                        return i
                    S.add("pe", mmo2, reads=[*WR[so0], *WR[so1]] + [f"at{k}" for k in range(8)],
                          writes=[f"pb{p}"])

                def post_wo(b):
                    p = wps[b]
                    post_norm_residual(pb[p], f"pb{p}", xa[b], xs[b], gpost[0], "gpost0")

                def norm_wo(b):
                    norm_a(xa[b], xs[b], b)
                    norm_b(b, 16)
                pipeline3(4, emit_wo, post_wo, norm_wo)
                load_gpost(2048, True)

                def post_b(b, tb=tb, ti=ti, xa=xa, xs=xs):
                    r0 = tb * TT + b * 128
                    o = dma("sp", out[r0:r0 + 128, :], xa[b], [xs[b]], [f"out{tb * 4 + b}", f"osq{b}"], f"ost{b}")
                    store_ops.append(o)
                    refill(ti, b)

                def norm_next(b, ti=ti):
                    if ti + 1 < len(all_tiles):
                        first_norm(ti + 1, blocks=[b])
                ffn_tile(1, xa, xs, gpost[0], "gpost0", post_hook=post_b, norm_fn=norm_next)
                if tb + 1 < len(tilesB):
                    load_gpost(1024, False)

        S.finalize()
        esems = {n: es.enter_context(nc.semaphore(f"sem_{n}")) for n in ("pe", "act", "dve", "pool", "sp")}
        dsems = {k: es.enter_context(nc.semaphore(f"dsem_{k}")) for k in S.dma_keys}
        block = es.enter_context(nc.Block())

        @block.sync
        def _(e):
            S.emit("sp", e, esems, dsems, final_wait=store_ops)

        @block.gpsimd
        def _(e):
            S.emit("pool", e, esems, dsems)

        @block.scalar
        def _(e):
            S.emit("act", e, esems, dsems)

        @block.vector
        def _(e):
            S.emit("dve", e, esems, dsems)

        @block.tensor
        def _(e):
            S.emit("pe", e, esems, dsems)
    return nc


def _rope_tables():
    rows = SEQ // 64
    row = np.repeat(np.arange(rows, dtype=np.float32), 64)
    col = np.tile(np.arange(64, dtype=np.float32), rows)
    inv_freq = (np.float32(10000.0) ** (-np.arange(16, dtype=np.float32) / np.float32(16))).astype(np.float32)
    ang = np.stack([row[:, None] * inv_freq, col[:, None] * inv_freq], axis=1).astype(np.float32)
    cos = np.cos(ang).astype(np.float32)
    sin = np.sin(ang).astype(np.float32)
    cos_d = np.zeros((64, SEQ), np.float32)
    sin_d = np.zeros((64, SEQ), np.float32)
    for axis in range(2):
        for half in range(2):
            for f in range(16):
                d = axis * 32 + half * 16 + f
                cos_d[d] = cos[:, axis, f]
                sin_d[d] = sin[:, axis, f] * (-1.0 if half == 0 else 1.0)
    return np.concatenate([cos_d, cos_d], 0), np.concatenate([sin_d, sin_d], 0)


def _perm64():
    d = np.arange(64)
    return np.where((d % 32) < 16, d + 16, d - 16)


_PROGRAM_CACHE = {}


def kernel(**inputs):
    f = lambda k: np.asarray(inputs[k], dtype=np.float32)
    x = f("x")
    memv = f("mem")
    w_in = f("w_in")[0]
    perm = _perm64()
    qcols = np.concatenate([np.concatenate([np.arange(64) + (c + 4 * a) * 64 for a in range(2)]) for c in range(4)])
    qcols_p = np.concatenate([np.concatenate([perm + (c + 4 * a) * 64 for a in range(2)]) for c in range(4)])
    w_q = np.ascontiguousarray(np.concatenate([w_in[:, qcols], w_in[:, qcols_p]], axis=1))
    kc = 512 + np.arange(128)
    kc_p = 512 + np.concatenate([perm, perm + 64])
    w_kkv = np.ascontiguousarray(np.concatenate([w_in[:, kc], w_in[:, kc_p], w_in[:, 640:768],
                                                 w_in[:, 640:768]], axis=1))
    w_gu = np.ascontiguousarray(w_in[:, 768:1280])
    w_gv = np.ascontiguousarray(w_in[:, 1280:1792])
    w_qm = np.ascontiguousarray(w_in[:, 1792:2304])
    wbg = f("w_branch_gate")[0]
    bbg = f("b_branch_gate")[0]
    w_bg = np.ascontiguousarray(np.concatenate(
        [wbg[:, r * 1024 + c * 128: r * 1024 + (c + 1) * 128] for c in range(8) for r in range(3)], axis=1))
    wpa = f("w_proj_attn")[0][qcols]
    wps = [wpa, f("w_proj_gmlp")[0], f("w_proj_mem")[0]]
    w_pr = np.ascontiguousarray(np.concatenate(
        [wps[r][:, c * 128:(c + 1) * 128] for c in range(8) for r in range(3)], axis=1))
    cols = np.zeros((128, 64), np.float32)
    cols[:, 0:8] = f("ffn1_pre")[0].reshape(8, 128).T
    cols[:, 8:16] = f("mix_pre")[0].reshape(8, 128).T
    cols[:, 16:24] = f("ffn2_pre")[0].reshape(8, 128).T
    cols[:, 24:32] = f("mem_norm")[0].reshape(8, 128).T
    for c in range(8):
        for r in range(3):
            cols[:, 32 + c * 3 + r] = bbg[r * 1024 + c * 128: r * 1024 + (c + 1) * 128]
    gq, gk = f("q_norm")[0], f("k_norm")[0]
    cols[:, 56] = np.concatenate([gq, gq])
    cols[:, 57] = np.concatenate([gq[perm], gq[perm]])
    cols[:, 58] = np.concatenate([gk, gk])
    cols[:, 59] = np.concatenate([gk[perm], gk[perm]])
    rows = np.concatenate([f("ffn1_post")[0], f("mix_post")[0], f("ffn2_post")[0], f("gmlp_v_norm")[0],
                           f("gmlp_b_s")[0].reshape(-1)])[None, :].astype(np.float32)
    wsT = np.ascontiguousarray(np.transpose(f("gmlp_w_s")[0], (2, 0, 1)).reshape(128, 512))
    cos_t, sin_t = _rope_tables()
    shared = dict(
        wg1=f("ffn1_w_gate")[0], wu1=f("ffn1_w_up")[0], wd1=f("ffn1_w_down")[0],
        wg2=f("ffn2_w_gate")[0], wu2=f("ffn2_w_up")[0], wd2=f("ffn2_w_down")[0],
        w_kkv=w_kkv, w_q=w_q, w_gu=w_gu, w_gv=w_gv, w_qm=w_qm, w_bg=w_bg, w_pr=w_pr,
        w_out=f("w_out")[0], w_mkv=f("w_mem_kv")[0], cols=cols, rows=rows, wsT=wsT)
    shared = {k: np.ascontiguousarray(v) for k, v in shared.items()}
    in_maps = []
    for c in range(8):
        b, hf = c // 2, c % 2
        own = slice(hf * OWN, (hf + 1) * OWN)
        oth = slice((1 - hf) * OWN, (2 - hf) * OWN)
        xin = np.ascontiguousarray(np.concatenate([x[b, own], x[b, oth]], axis=0))
        rp = np.ascontiguousarray(np.stack([np.concatenate([cos_t[:, own], cos_t[:, oth]], axis=1),
                                            np.concatenate([sin_t[:, own], sin_t[:, oth]], axis=1)], axis=0))
        m = dict(shared)
        m.update(xin=xin, mem=np.ascontiguousarray(memv[b]), rope=rp)
        in_maps.append(m)
    if "full" not in _PROGRAM_CACHE:
        _PROGRAM_CACHE["full"] = build_program("full")
    nc = _PROGRAM_CACHE["full"]
    res = run_bass_kernel_spmd(nc, in_maps, core_ids=list(range(8)))
    outp = np.empty((4, SEQ, D), np.float32)
    for c in range(8):
        b, hf = c // 2, c % 2
        outp[b, hf * OWN:(hf + 1) * OWN] = res.results[c]["out"]
    return outp
```
